# Optimizing a Trainium2 kernel written in Bass

```python
import jax, jax.numpy as jnp
from jax import lax
import numpy as np

D_MODEL = 1024
BATCH = 16
SEQ = 4096
DEPTH = 4

D_MIX = D_MODEL
RET_HEADS = 4
RET_HEAD_DIM = 128
RET_WIDTH = RET_HEADS * RET_HEAD_DIM
RET_CHUNK = 128
GDN_HEADS = 4
GDN_HEAD_DIM = 128
GDN_WIDTH = GDN_HEADS * GDN_HEAD_DIM
GDN_CHUNK = 64
CONV_K = 5
D_FF = 2816
MEM_LEN = 256
XATTN_HEADS = 4
XATTN_HEAD_DIM = D_MODEL // XATTN_HEADS
ROPE_BASE = 10000.0
NORM_EPS = 1e-6
N_SUBLAYERS = 4
IN_COLS = 4 * RET_WIDTH + 4 * GDN_WIDTH + 4 * GDN_HEADS
IN_SPLITS = (RET_WIDTH, 2 * RET_WIDTH, 3 * RET_WIDTH, 4 * RET_WIDTH,
             4 * RET_WIDTH + 3 * GDN_WIDTH, 4 * RET_WIDTH + 4 * GDN_WIDTH,
             4 * RET_WIDTH + 4 * GDN_WIDTH + 2 * GDN_HEADS)

kernel_name = "hybrid_retention_gdn_macaron_encoder"


def rms_norm(x, w):
    xf = x.astype(jnp.float32)
    y = xf * lax.rsqrt(jnp.mean(xf * xf, axis=-1, keepdims=True) + NORM_EPS)
    return (y * w.astype(jnp.float32)).astype(x.dtype)


def swiglu(u, w_gate, w_up, w_down):
    return (jax.nn.silu(u @ w_gate) * (u @ w_up)) @ w_down


def rotary_tables(positions):
    inv_freq = ROPE_BASE ** (-jnp.arange(0, RET_HEAD_DIM, 2, dtype=jnp.float32) / RET_HEAD_DIM)
    ang = positions.astype(jnp.float32)[..., None] * inv_freq
    return jnp.cos(ang)[:, :, None, :], jnp.sin(ang)[:, :, None, :]


def rotary(x, cos, sin):
    half = x.shape[-1] // 2
    x1, x2 = x[..., :half], x[..., half:]
    return jnp.concatenate([x1 * cos - x2 * sin, x1 * sin + x2 * cos], axis=-1)


def flip_seq(t):
    return jnp.flip(t, axis=2)


def retention_direction(q, k, v, log_gamma, strict):
    b, h, s, d = q.shape
    n = s // RET_CHUNK
    q, k, v = (t.reshape(b, h, n, RET_CHUNK, d) for t in (q, k, v))
    pos = jnp.arange(RET_CHUNK, dtype=jnp.float32)
    rel = pos[:, None] - pos[None, :]
    mask = rel > 0 if strict else rel >= 0
    lg = log_gamma[:, None, None]
    dmat = jnp.where(mask, jnp.exp(jnp.where(mask, rel, 0.0) * lg), 0.0)
    scores = jnp.einsum('bhnid,bhnjd->bhnij', q, k) * dmat[:, None]
    o_intra = jnp.einsum('bhnij,bhnje->bhnie', scores, v)
    lgv = log_gamma[:, None]
    k_dec = k * jnp.exp((RET_CHUNK - 1.0 - pos) * lgv)[:, None, :, None]
    chunk_kv = jnp.einsum('bhncd,bhnce->bhnde', k_dec, v)
    chunk_decay = jnp.exp(RET_CHUNK * log_gamma)[None, :, None, None]

    def step(state, kv_c):
        return state * chunk_decay + kv_c, state

    _, prev = lax.scan(step, jnp.zeros((b, h, d, d), q.dtype), jnp.moveaxis(chunk_kv, 2, 0))
    prev = jnp.moveaxis(prev, 0, 2)
    q_dec = q * jnp.exp((pos + 1.0) * lgv)[:, None, :, None]
    o_inter = jnp.einsum('bhncd,bhnde->bhnce', q_dec, prev)
    return (o_intra + o_inter).reshape(b, h, s, d)


def retention_group(rq, rk, rv, rg, cos, sin, log_gamma):
    b, s, _ = rq.shape
    heads = lambda t: t.reshape(b, s, RET_HEADS, RET_HEAD_DIM)
    q = rotary(heads(rq), cos, sin)
    k = rotary(heads(rk), cos, sin) * RET_HEAD_DIM ** -0.5
    v = heads(rv)
    q, k, v = (jnp.swapaxes(t, 1, 2).astype(jnp.float32) for t in (q, k, v))
    lgam = log_gamma.astype(jnp.float32)
    fwd = retention_direction(q, k, v, lgam[0], strict=False)
    bwd = retention_direction(flip_seq(q), flip_seq(k), flip_seq(v), lgam[1], strict=True)
    o = fwd + flip_seq(bwd)
    mu = jnp.mean(o, axis=-1, keepdims=True)
    var = jnp.mean(jnp.square(o - mu), axis=-1, keepdims=True)
    o = (o - mu) * lax.rsqrt(var + NORM_EPS)
    o = jnp.swapaxes(o, 1, 2).reshape(b, s, RET_WIDTH).astype(rg.dtype)
    return jax.nn.silu(rg) * o


def gdn_direction(q, k, v, g, beta):
    b, h, s, dk = q.shape
    dv = v.shape[-1]
    n = s // GDN_CHUNK
    c = GDN_CHUNK
    q, k, v = (t.reshape(b, h, n, c, t.shape[-1]) for t in (q, k, v))
    g = jnp.cumsum(g.reshape(b, h, n, c), axis=-1)
    beta = beta.reshape(b, h, n, c)
    tril = jnp.tril(jnp.ones((c, c), bool))
    strict = jnp.tril(jnp.ones((c, c), bool), -1)
    decay = jnp.exp(jnp.where(tril, g[..., :, None] - g[..., None, :], -jnp.inf))
    k_beta = k * beta[..., None]
    lower = jnp.where(strict, jnp.einsum('bhnid,bhnjd->bhnij', k_beta, k) * decay, 0.0)
    rhs = jnp.concatenate([v * beta[..., None], k_beta * jnp.exp(g)[..., None]], axis=-1)
    sol = lax.linalg.triangular_solve(lower, rhs, left_side=True, lower=True, unit_diagonal=True)
    u, w = sol[..., :dv], sol[..., dv:]
    qk = jnp.einsum('bhnid,bhnjd->bhnij', q, k) * decay
    q_g = q * jnp.exp(g)[..., None]
    k_tail = k * jnp.exp(g[..., -1:] - g)[..., None]
    g_last = jnp.exp(g[..., -1])

    def step(state, xs):
        u_c, w_c, qk_c, qg_c, kt_c, gl_c = xs
        v_new = u_c - w_c @ state
        o_c = qg_c @ state + qk_c @ v_new
        state = state * gl_c[..., None, None] + jnp.swapaxes(kt_c, -1, -2) @ v_new
        return state, o_c

    xs = tuple(jnp.moveaxis(t, 2, 0) for t in (u, w, qk, q_g, k_tail, g_last))
    _, o = lax.scan(step, jnp.zeros((b, h, dk, dv), jnp.float32), xs)
    return jnp.moveaxis(o, 0, 2).reshape(b, h, s, dv)


def centred_depthwise_conv(x, w):
    return lax.conv_general_dilated(
        x, w[:, None, :], window_strides=(1,),
        padding=[((CONV_K - 1) // 2, CONV_K // 2)],
        dimension_numbers=('NWC', 'WIO', 'NWC'),
        feature_group_count=x.shape[-1])


def gdn_group(qkv, z, a, bl, conv_w, a_log, dt_bias, norm_w):
    b, s, _ = qkv.shape
    qkv = jax.nn.silu(centred_depthwise_conv(qkv, conv_w))
    q, k, v = jnp.split(qkv, 3, axis=-1)
    heads = lambda t: jnp.swapaxes(t.reshape(b, s, GDN_HEADS, GDN_HEAD_DIM), 1, 2).astype(jnp.float32)
    q, k, v = heads(q), heads(k), heads(v)
    l2n = lambda t: t * lax.rsqrt(jnp.sum(t * t, axis=-1, keepdims=True) + NORM_EPS)
    q = l2n(q) * GDN_HEAD_DIM ** -0.5
    k = l2n(k)
    dir_heads = lambda t: jnp.transpose(t.reshape(b, s, 2, GDN_HEADS), (2, 0, 3, 1)).astype(jnp.float32)
    a, bl = dir_heads(a), dir_heads(bl)
    g = -jnp.exp(a_log.astype(jnp.float32))[:, None, :, None] * jax.nn.softplus(
        a + dt_bias.astype(jnp.float32)[:, None, :, None])
    beta = jax.nn.sigmoid(bl)
    fwd = gdn_direction(q, k, v, g[0], beta[0])
    bwd = gdn_direction(flip_seq(q), flip_seq(k), flip_seq(v), flip_seq(g[1]), flip_seq(beta[1]))
    o = fwd + flip_seq(bwd)
    o = o * lax.rsqrt(jnp.mean(o * o, axis=-1, keepdims=True) + NORM_EPS) * norm_w.astype(jnp.float32)
    o = jnp.swapaxes(o, 1, 2).reshape(b, s, GDN_WIDTH).astype(z.dtype)
    return o * jax.nn.silu(z)


def parallel_head_groups(u, cos, sin, w_in, conv_w, ret_log_gamma, a_log, dt_bias, gdn_norm_w, w_out):
    proj = u @ w_in
    rq, rk, rv, rg, qkv, z, a, bl = jnp.split(proj, IN_SPLITS, axis=-1)
    o_ret = retention_group(rq, rk, rv, rg, cos, sin, ret_log_gamma)
    o_gdn = gdn_group(qkv, z, a, bl, conv_w, a_log, dt_bias, gdn_norm_w)
    return jnp.concatenate([o_ret, o_gdn], axis=-1) @ w_out


def memory_cross_attention(u, mem_n, wq, wk, wv, wo):
    b, s, _ = u.shape
    m = mem_n.shape[1]
    q = (u @ wq).reshape(b, s, XATTN_HEADS, XATTN_HEAD_DIM)
    k = (mem_n @ wk).reshape(b, m, XATTN_HEADS, XATTN_HEAD_DIM)
    v = (mem_n @ wv).reshape(b, m, XATTN_HEADS, XATTN_HEAD_DIM)
    scores = jnp.einsum('bshd,bmhd->bhsm', q, k).astype(jnp.float32) * XATTN_HEAD_DIM ** -0.5
    p = jax.nn.softmax(scores, axis=-1).astype(v.dtype)
    o = jnp.einsum('bhsm,bmhd->bshd', p, v).reshape(b, s, D_MODEL)
    return o @ wo


def setup_inputs(seed: int = 0) -> dict:
    key = jax.random.key(seed)
    ks = jax.random.split(key, 24)
    nrm = lambda k, shape: jax.random.normal(k, shape, jnp.float32)
    dense = lambda k, shape, fan_in: nrm(k, shape) * fan_in ** -0.5
    gain = lambda k, shape: 1.0 + 0.02 * nrm(k, shape)
    x = nrm(ks[0], (BATCH, SEQ, D_MODEL))
    mem = nrm(ks[1], (BATCH, MEM_LEN, D_MODEL))
    positions = jnp.broadcast_to(jnp.arange(SEQ, dtype=jnp.int32)[None, :], (BATCH, SEQ))
    base = jnp.log1p(-(2.0 ** (-5.0 - jnp.arange(RET_HEADS, dtype=jnp.float32))))
    ret_log_gamma = base * (1.0 + 0.05 * nrm(ks[10], (DEPTH, 2, RET_HEADS)))
    gdn_a_log = jnp.log(jax.random.uniform(ks[11], (DEPTH, 2, GDN_HEADS), jnp.float32, 1.0, 16.0))
    dt = jnp.exp(jax.random.uniform(ks[12], (DEPTH, 2, GDN_HEADS), jnp.float32,
                                    float(np.log(1e-3)), float(np.log(1e-1))))
    gdn_dt_bias = dt + jnp.log(-jnp.expm1(-dt))
    return {
        "x": x,
        "mem": mem,
        "positions": positions,
        "norm_pre": gain(ks[2], (DEPTH, N_SUBLAYERS, D_MODEL)),
        "norm_post": gain(ks[3], (DEPTH, N_SUBLAYERS, D_MODEL)),
        "mem_norm": gain(ks[4], (DEPTH, D_MODEL)),
        "ffn1_gate": dense(ks[5], (DEPTH, D_MODEL, D_FF), D_MODEL),
        "ffn1_up": dense(ks[6], (DEPTH, D_MODEL, D_FF), D_MODEL),
        "ffn1_down": dense(ks[7], (DEPTH, D_FF, D_MODEL), D_FF),
        "w_in": dense(ks[8], (DEPTH, D_MODEL, IN_COLS), D_MODEL),
        "gdn_conv": dense(ks[9], (DEPTH, CONV_K, 3 * GDN_WIDTH), CONV_K),
        "ret_log_gamma": ret_log_gamma,
        "gdn_a_log": gdn_a_log,
        "gdn_dt_bias": gdn_dt_bias,
        "gdn_norm": gain(ks[13], (DEPTH, GDN_HEAD_DIM)),
        "w_out": dense(ks[14], (DEPTH, D_MIX, D_MODEL), D_MIX),
        "xattn_q": dense(ks[15], (DEPTH, D_MODEL, D_MODEL), D_MODEL),
        "xattn_k": dense(ks[16], (DEPTH, D_MODEL, D_MODEL), D_MODEL),
        "xattn_v": dense(ks[17], (DEPTH, D_MODEL, D_MODEL), D_MODEL),
        "xattn_o": dense(ks[18], (DEPTH, D_MODEL, D_MODEL), D_MODEL),
        "ffn2_gate": dense(ks[19], (DEPTH, D_MODEL, D_FF), D_MODEL),
        "ffn2_up": dense(ks[20], (DEPTH, D_MODEL, D_FF), D_MODEL),
        "ffn2_down": dense(ks[21], (DEPTH, D_FF, D_MODEL), D_FF),
    }


def reference(x, mem, positions, norm_pre, norm_post, mem_norm, ffn1_gate, ffn1_up, ffn1_down,
              w_in, gdn_conv, ret_log_gamma, gdn_a_log, gdn_dt_bias, gdn_norm, w_out,
              xattn_q, xattn_k, xattn_v, xattn_o, ffn2_gate, ffn2_up, ffn2_down):
    cos, sin = rotary_tables(positions)
    for l in range(DEPTH):
        h = swiglu(rms_norm(x, norm_pre[l, 0]), ffn1_gate[l], ffn1_up[l], ffn1_down[l])
        x = x + 0.5 * rms_norm(h, norm_post[l, 0])
        h = parallel_head_groups(rms_norm(x, norm_pre[l, 1]), cos, sin, w_in[l], gdn_conv[l],
                                 ret_log_gamma[l], gdn_a_log[l], gdn_dt_bias[l], gdn_norm[l], w_out[l])
        x = x + rms_norm(h, norm_post[l, 1])
        h = memory_cross_attention(rms_norm(x, norm_pre[l, 2]), rms_norm(mem, mem_norm[l]),
                                   xattn_q[l], xattn_k[l], xattn_v[l], xattn_o[l])
        x = x + rms_norm(h, norm_post[l, 2])
        h = swiglu(rms_norm(x, norm_pre[l, 3]), ffn2_gate[l], ffn2_up[l], ffn2_down[l])
        x = x + 0.5 * rms_norm(h, norm_post[l, 3])
    return x
```

```python
import math
import os
import numpy as np
import concourse.bass as bass
import concourse.mybir as mybir

F32 = mybir.dt.float32
BF16 = mybir.dt.bfloat16
I32 = mybir.dt.int32
ALU = mybir.AluOpType
AF = mybir.ActivationFunctionType
AX = mybir.AxisListType

COMPUTE = ("tensor", "vector", "scalar", "gpsimd")
SAME_ENGINE_SYNC = True


def _region(ap):
    t = ap.tensor
    es = mybir.dt.size(ap.dtype)
    pat = [(st * es, n) for st, n in ap.ap]
    off = ap.offset * es
    if type(t).__name__ == "DRamTensorHandle":
        lo = off
        hi = off
        for st, n in pat:
            if st >= 0:
                hi += st * (n - 1)
            else:
                lo += st * (n - 1)
        return (t.name, 0, 1, lo, hi + es)
    if type(t).__name__ == "PSumTensorHandle":
        return (t.name, 0, 128, 0, 2048)
    pst, pn = pat[0]
    if pst == 0:
        pst = 1 << 40
    p0 = off // pst if pst < (1 << 40) else 0
    f = off - p0 * pst if pst < (1 << 40) else off
    lo = f
    hi = f
    for st, n in pat[1:]:
        if st >= 0:
            hi += st * (n - 1)
        else:
            lo += st * (n - 1)
    return (t.name, p0, p0 + pn, lo, hi + es)


def _overlap(a, b):
    return a[1] < b[2] and b[1] < a[2] and a[3] < b[4] and b[3] < a[4]


def _covers(a, b):
    return a[1] <= b[1] and a[2] >= b[2] and a[3] <= b[3] and a[4] >= b[4]


class Op:
    __slots__ = ("eng", "fn", "deps", "idx", "marked", "cnt", "is_dma", "semkey", "dval", "kind")

    def __init__(self, eng, fn, is_dma=False, semkey=None, kind=""):
        self.eng = eng
        self.fn = fn
        self.deps = {}
        self.marked = False
        self.cnt = 0
        self.is_dma = is_dma
        self.semkey = semkey
        self.dval = 0
        self.kind = kind


class FW:
    def __init__(self, nc):
        self.nc = nc
        self.ops = []
        self.recs = {}
        self.dma_cnt = {}
        self.dma_last = {}
        self.n_auto = 0

    def sb(self, name, shape, dt=F32):
        return self.nc.alloc_sbuf_tensor(name, list(shape), dt)

    def ps(self, name, shape, dt=F32):
        return self.nc.alloc_psum_tensor(name, list(shape), dt)

    def dram(self, name, shape, dt=F32, kind="Internal"):
        return self.nc.dram_tensor(name, list(shape), dt, kind=kind)

    def _access(self, op, ap, is_write):
        reg = _region(ap)
        recs = self.recs.setdefault(reg[0], [])
        is_psum = type(ap.tensor).__name__ == "PSumTensorHandle"
        new = []
        for (r, o, w) in recs:
            keep = True
            if _overlap(r, reg) and (w or is_write or (is_psum and o.eng != op.eng)):
                if o is not op:
                    self._dep(op, o)
                if is_write and _covers(reg, r):
                    keep = False
            if keep and (not is_write) and (not w) and o.eng == op.eng and (not o.is_dma) and (not op.is_dma) and r == reg:
                keep = False
            if keep:
                new.append((r, o, w))
        new.append((reg, op, is_write))
        self.recs[reg[0]] = new

    def _dep(self, op, o):
        if o.is_dma:
            key = ("dma", id(o))
            op.deps[key] = o
        else:
            if o.eng == op.eng and not op.is_dma and (o.eng == "tensor" or not SAME_ENGINE_SYNC):
                return
            prev = op.deps.get(o.eng)
            if prev is None or prev.idx < o.idx:
                op.deps[o.eng] = o

    def add(self, eng, fn, reads=(), writes=(), is_dma=False, semkey=None, kind=""):
        op = Op(eng, fn, is_dma, semkey, kind)
        op.idx = len(self.ops)
        self._apply_fence(op)
        for ap in reads:
            self._access(op, ap, False)
        for ap in writes:
            self._access(op, ap, True)
        if is_dma:
            last = self.dma_last.get(semkey)
            if last is not None:
                op.deps[("dma", id(last))] = last
            self.dma_cnt[semkey] = self.dma_cnt.get(semkey, 0) + 1
            op.dval = 16 * self.dma_cnt[semkey]
            self.dma_last[semkey] = op
        self.ops.append(op)
        return op

    def fence(self):
        last = {}
        for op in self.ops:
            if not op.is_dma:
                last[op.eng] = op
        dl = list(self.dma_last.values())
        self._fence = (last, dl)
        self._fence_pending = {e: True for e in ("tensor", "vector", "scalar", "gpsimd", "sync")}

    def _apply_fence(self, op):
        f = getattr(self, "_fence", None)
        if f is None or not self._fence_pending.get(op.eng):
            return
        self._fence_pending[op.eng] = False
        last, dl = f
        for e, o in last.items():
            if e == op.eng and not op.is_dma:
                continue
            prev = op.deps.get(e)
            if prev is None or prev.idx < o.idx:
                op.deps[e] = o
        for d in dl:
            op.deps[("dma", id(d))] = d

    def I(self, eng, meth, *, out=None, outs=(), ins=(), **kw):
        reads = list(ins)
        writes = list(outs)
        kwargs = dict(kw)
        if out is not None:
            kwargs["out"] = out
            writes.append(out)
        for k, v in kw.items():
            if type(v).__name__ == "AP":
                if k in ("accum_out", "ap"):
                    writes.append(v)
                else:
                    reads.append(v)
        e = getattr(self.nc, eng)

        def fn():
            return getattr(e, meth)(**kwargs)
        return self.add(eng, fn, reads, writes, kind=meth)

    def matmul(self, out, lhsT, rhs, start=True, stop=True, **kw):
        e = self.nc.tensor

        def fn():
            return e.matmul(out, lhsT, rhs, start=start, stop=stop, **kw)
        return self.add("tensor", fn, [lhsT, rhs], [out], kind="matmul")

    def transpose(self, out, in_, ident):
        e = self.nc.tensor

        def fn():
            return e.transpose(out, in_, ident)
        return self.add("tensor", fn, [in_, ident], [out], kind="transpose")

    def dma(self, q, out, in_, semkey=None, **kw):
        e = getattr(self.nc, q)
        if semkey is None:
            semkey = out.tensor.name.split("__u")[0]

        def fn():
            return e.dma_start(out=out, in_=in_, **kw)
        return self.add(q, fn, [in_], [out], is_dma=True, semkey=semkey, kind="dma")

    def finalize(self):
        from contextlib import ExitStack
        nc = self.nc
        ENGS = ("tensor", "vector", "scalar", "gpsimd", "sync")
        for op in self.ops:
            for d in op.deps.values():
                d.marked = True
        cnt = {e: 0 for e in ENGS}
        for op in self.ops:
            if not op.is_dma and op.marked:
                cnt[op.eng] += 1
                op.cnt = cnt[op.eng]
        semkeys = {}
        waited = {e: {} for e in ENGS}
        nwait = 0
        per_eng = {e: [] for e in ENGS}
        for op in self.ops:
            w = waited[op.eng]
            waits = []
            for key, d in op.deps.items():
                if d.is_dma:
                    sk = ("dma", d.semkey)
                    val = d.dval
                else:
                    sk = ("eng", d.eng)
                    val = d.cnt
                if w.get(sk, 0) >= val:
                    continue
                w[sk] = val
                semkeys.setdefault(sk, len(semkeys))
                waits.append((sk, val))
                nwait += 1
            if op.is_dma:
                semkeys.setdefault(("dma", op.semkey), len(semkeys))
            elif op.marked:
                semkeys.setdefault(("eng", op.eng), len(semkeys))
            per_eng[op.eng].append((op, waits))
        final = []
        for key, last in self.dma_last.items():
            sk = ("dma", key)
            if waited["sync"].get(sk, 0) < last.dval:
                final.append((sk, last.dval))
        self.stats = dict(n_ops=len(self.ops), n_wait=nwait, n_sems=len(semkeys),
                          per_eng={e: len(v) for e, v in per_eng.items()})
        with ExitStack() as st:
            sems = {}
            for sk, i in semkeys.items():
                sems[sk] = st.enter_context(nc.semaphore("s%d" % i))
            block = st.enter_context(nc.Block())

            def body(ename):
                def f(eng):
                    for op, waits in per_eng[ename]:
                        for sk, val in waits:
                            eng.wait_ge(sems[sk], val)
                        inst = op.fn()
                        if op.is_dma:
                            inst.then_inc(sems[("dma", op.semkey)], 16)
                        elif op.marked:
                            inst.then_inc(sems[("eng", op.eng)], 1)
                    if ename == "sync":
                        for sk, val in final:
                            eng.wait_ge(sems[sk], val)
                return f
            for ename in ENGS:
                if per_eng[ename] or ename == "sync":
                    getattr(block, ename)(body(ename))
        return self.stats


import numpy as np
from contextlib import ExitStack
import concourse.bass as bass
import concourse.mybir as mybir
from concourse.bass_utils import run_bass_kernel_spmd

D = 1024
DFF = 2816
NFF = DFF // 128
INC = 4112
MEM = 256
EPS = 1e-6
TT = 512


class Cfg:
    def __init__(self, NS=2, S=4096, DEPTH=4, phases=("ffn1", "mix", "xattn", "ffn2")):
        self.NS, self.S, self.DEPTH, self.phases = NS, S, DEPTH, phases
        self.NT = NS * S


WNAMES = [("ffn1_gate", D, DFF), ("ffn1_up", D, DFF), ("ffn1_down", DFF, D), ("w_in", D, INC),
          ("w_out", D, D), ("xattn_q", D, D), ("xattn_k", D, D), ("xattn_v", D, D), ("xattn_o", D, D),
          ("ffn2_gate", D, DFF), ("ffn2_up", D, DFF), ("ffn2_down", DFF, D)]


class K:
    def __init__(self, cfg):
        self.cfg = cfg
        nc = self.nc = bass.Bass("TRN2", target_bir_lowering=False)
        fw = self.fw = FW(nc)
        L = cfg.DEPTH
        NT = cfg.NT
        ein = lambda n, sh, dt=F32: nc.dram_tensor(n, list(sh), dt, kind="ExternalInput").ap()
        self.x_in = ein("x", [NT, D])
        self.mem = ein("mem", [cfg.NS * MEM, D])
        self.pos = ein("positions", [cfg.NS * cfg.S], I32)
        self.norm_pre = ein("norm_pre", [L * 4, D])
        self.norm_post = ein("norm_post", [L * 4, D])
        self.mem_norm = ein("mem_norm", [L, D])
        self.w32 = {}
        self.wbf = {}
        for n, r, c in WNAMES:
            self.w32[n] = ein(n, [L * r, c])
            self.wbf[n] = nc.dram_tensor(n + "_bf", [L * r, c], BF16, kind="Internal").ap()
        self.gdn_conv = ein("gdn_conv", [L * 5, 1536])
        self.ret_lg = ein("ret_log_gamma", [L, 8])
        self.a_log = ein("gdn_a_log", [L, 8])
        self.dt_bias = ein("gdn_dt_bias", [L, 8])
        self.gdn_norm = ein("gdn_norm", [L, 128])
        self.y = nc.dram_tensor("y", [NT, D], F32, kind="ExternalOutput").ap()
        self.bank = [fw.ps("bank%d" % i, [128, 512]) for i in range(8)]
        self.st = ExitStack()
        self.identf = fw.sb("identf", [128, 128])
        self.identb = fw.sb("identb", [128, 128], BF16)
        self.onesf = fw.sb("onesf", [128, 128])
        self.zerosf = fw.sb("zerosf", [128, 128])
        self.junk = fw.sb("junk", [128, 1024])
        self.epsc = fw.sb("epsc", [128, 1])
        self.eps4 = fw.sb("eps4", [128, 1])
        self.mhalf = fw.sb("mhalf", [128, 512])
        self.consts()
        self.mix_consts()

    def uid(self):
        self._uid = getattr(self, "_uid", 0) + 1
        return self._uid

    def V(self, meth, **kw):
        return self.fw.I("vector", meth, **kw)

    def A(self, **kw):
        return self.fw.I("scalar", "activation", **kw)

    def G(self, meth, **kw):
        return self.fw.I("gpsimd", meth, **kw)

    def consts(self):
        fw = self.fw
        self.G("memset", ap=self.onesf[:], constant=1.0)
        self.G("memset", ap=self.zerosf[:], constant=0.0)
        self.G("memset", ap=self.epsc[:], constant=EPS)
        self.G("memset", ap=self.eps4[:], constant=EPS * 4.0)
        self.G("memset", ap=self.mhalf[:], constant=-0.5)
        self.G("affine_select", out=self.identf[:], in_=self.onesf[:], pattern=[[1, 128]],
               compare_op=ALU.is_equal, fill=0.0, base=0, channel_multiplier=-1)
        self.V("tensor_copy", out=self.identb[:], in_=self.identf[:])

    def cast_weights(self, l):
        for n, r, c in WNAMES:
            src = self.w32[n][l * r:(l + 1) * r, :]
            dst = self.wbf[n][l * r:(l + 1) * r, :]
            for r0 in range(0, r, 1024):
                r1 = min(r, r0 + 1024)
                for c0 in range(0, c, 2048):
                    c1 = min(c, c0 + 2048)
                    self.fw.dma("gpsimd", dst[r0:r1, c0:c1], src[r0:r1, c0:c1], semkey=("cast", n))

    def alloc_common(self, st, nxt=2, post=True):
        nc = self.nc
        sb = lambda n, sh, dt=F32: st.enter_context(nc.sbuf_tensor(n + "__u%d" % self.uid(), list(sh), dt))
        self.xt = [sb("xt%d" % i, [128, 4, D]) for i in range(nxt)]
        self.nxt = nxt
        self.wpre = sb("wpre", [128, D])
        self.wpost = sb("wpost", [128, D])
        self.ub = [sb("ub%d" % i, [128, D], BF16) for i in range(2)]
        self.uT = sb("uT", [128, 8, TT], BF16)
        self.ss = sb("ss", [128, 8])
        self.rs = sb("rs", [128, 8])
        self.ss2 = sb("ss2", [128, 4])
        self.rs2 = sb("rs2", [128, 4])
        if post:
            self.tpost = [sb("tpost%d" % i, [128, D]) for i in range(2)]
        self.tile_i = 0

    def load_norm_w(self, l, sub):
        self.fw.dma("sync", self.wpre[:], self.norm_pre[l * 4 + sub].partition_broadcast(128))
        self.fw.dma("sync", self.wpost[:], self.norm_post[l * 4 + sub].partition_broadcast(128))

    def load_tile(self, row0, src=None):
        src = self.y if src is None else src
        xt = self.xt[self.tile_i % self.nxt]
        self.tile_i += 1
        self.fw.dma("sync", xt[:], src[row0:row0 + TT, :].rearrange("(s p) d -> p s d", p=128))
        return xt

    def rstd_of(self, ss, rs, n, scale):
        self.V("tensor_scalar", out=rs[:, 0:n], in0=ss[:, 0:n], scalar1=scale, scalar2=EPS, op0=ALU.mult, op1=ALU.add)
        self.G("tensor_tensor", out=rs[:, 0:n], in0=rs[:, 0:n], in1=self.mhalf[:, 0:n], op=ALU.pow)

    def norm_uT(self, xt, tbank):
        for s in range(4):
            self.A(out=self.junk[:], in_=xt[:, s, :], func=AF.Square, accum_out=self.ss[:, s:s + 1])
        self.rstd_of(self.ss, self.rs, 4, 1.0 / D)
        for s in range(4):
            ub = self.ub[s % 2]
            self.V("scalar_tensor_tensor", out=ub[:], in0=xt[:, s, :], scalar=self.rs[:, s:s + 1],
                   in1=self.wpre[:], op0=ALU.mult, op1=ALU.mult)
            tb = tbank[s % 2].bitcast(BF16)
            for j in range(8):
                self.fw.transpose(tb[:, j * 128:(j + 1) * 128], ub[:, j * 128:(j + 1) * 128], self.identb[:])
            self.A(out=self.uT[:, :, s * 128:(s + 1) * 128],
                   in_=tb[:, :].rearrange("p (j t) -> p j t", j=8), func=AF.Copy)

    def post_resid(self, xt, s, pb0, pb1, factor):
        self.A(out=self.junk[:, 0:512], in_=pb0[:], func=AF.Square, accum_out=self.ss2[:, 0:1])
        self.A(out=self.junk[:, 512:1024], in_=pb1[:], func=AF.Square, accum_out=self.ss2[:, 1:2])
        self.V("tensor_tensor", out=self.ss2[:, 2:3], in0=self.ss2[:, 0:1], in1=self.ss2[:, 1:2], op=ALU.add)
        f2 = 1.0 / (factor * factor)
        self.V("tensor_scalar", out=self.rs2[:, 0:1], in0=self.ss2[:, 2:3], scalar1=f2 / D, scalar2=EPS * f2, op0=ALU.mult, op1=ALU.add)
        self.G("tensor_tensor", out=self.rs2[:, 1:2], in0=self.rs2[:, 0:1], in1=self.mhalf[:, 0:1], op=ALU.pow)
        tp = self.tpost[s % 2]
        self.V("scalar_tensor_tensor", out=tp[:, 0:512], in0=pb0[:], scalar=self.rs2[:, 1:2],
               in1=self.wpost[:, 0:512], op0=ALU.mult, op1=ALU.mult)
        self.V("scalar_tensor_tensor", out=tp[:, 512:1024], in0=pb1[:], scalar=self.rs2[:, 1:2],
               in1=self.wpost[:, 512:1024], op0=ALU.mult, op1=ALU.mult)
        self.V("tensor_tensor", out=xt[:, s, :], in0=xt[:, s, :], in1=tp[:], op=ALU.add)

    def epsf(self, f2):
        key = "epsf%g" % f2
        if not hasattr(self, "_epsf"):
            self._epsf = {}
        if key not in self._epsf:
            t = self.fw.sb("epsf%d" % len(self._epsf), [128, 1])
            self.G("memset", ap=t[:], constant=EPS * f2)
            self._epsf[key] = t
        return self._epsf[key][:, 0:1]

    def store_tile(self, xt, row0):
        self.fw.dma("sync", self.y[row0:row0 + TT, :].rearrange("(s p) d -> p s d", p=128), xt[:],
                    semkey=("ystore", self.tile_i % 2))

    def ffn_phase(self, l, which):
        cfg, fw, nc = self.cfg, self.fw, self.nc
        pre = "ffn%d_" % which
        sub = 0 if which == 1 else 3
        fw.fence()
        with ExitStack() as st:
            sb = lambda n, sh, dt=F32: st.enter_context(nc.sbuf_tensor(n + "__u%d" % self.uid(), list(sh), dt))
            self.alloc_common(st)
            wd = sb("wd", [128, NFF, D], BF16)
            slab = [sb("slab%d" % i, [128, 2, 8, 256], BF16) for i in range(3)]
            h1T = sb("h1T", [128, NFF, TT], BF16)
            sg = [sb("sg%d" % i, [128, TT]) for i in range(2)]
            self.load_norm_w(l, sub)
            wdsrc = self.wbf[pre + "down"][l * DFF:(l + 1) * DFF, :].rearrange("(c p) n -> p c n", p=128)
            fw.dma("sync", wd[:, 0:11, :], wdsrc[:, 0:11, :])
            fw.dma("sync", wd[:, 11:22, :], wdsrc[:, 11:22, :], semkey="wd_b")
            wg = self.wbf[pre + "gate"][l * D:(l + 1) * D, :].rearrange("(k p) n -> p k n", p=128)
            wu = self.wbf[pre + "up"][l * D:(l + 1) * D, :].rearrange("(k p) n -> p k n", p=128)
            B = self.bank
            sl_i = 0
            nxt = self.load_tile(0)
            for t in range(cfg.NT // TT):
                xt = nxt
                self.norm_uT(xt, [B[6], B[7]])
                if t + 1 < cfg.NT // TT:
                    nxt = self.load_tile((t + 1) * TT)
                for sl in range(NFF // 2):
                    sbuf = slab[sl_i % 3]
                    sl_i += 1
                    fw.dma("sync", sbuf[:, 0, :, :], wg[:, :, sl * 256:(sl + 1) * 256], semkey=("slab", sl_i % 3, 0))
                    fw.dma("sync", sbuf[:, 1, :, :], wu[:, :, sl * 256:(sl + 1) * 256], semkey=("slab", sl_i % 3, 1))
                    for cc in range(2):
                        c = sl * 2 + cc
                        pg, pu = B[(c % 2) * 2], B[(c % 2) * 2 + 1]
                        for k in range(8):
                            fw.matmul(pg[:], sbuf[:, 0, k, cc * 128:(cc + 1) * 128], self.uT[:, k, :], start=(k == 0), stop=(k == 7))
                        for k in range(8):
                            fw.matmul(pu[:], sbuf[:, 1, k, cc * 128:(cc + 1) * 128], self.uT[:, k, :], start=(k == 0), stop=(k == 7))
                        self.A(out=sg[c % 2][:], in_=pg[:], func=AF.Tanh, scale=0.5)
                        self.V("scalar_tensor_tensor", out=sg[c % 2][:], in0=sg[c % 2][:], scalar=1.0, in1=pg[:], op0=ALU.add, op1=ALU.mult)
                        self.V("scalar_tensor_tensor", out=h1T[:, c, :], in0=sg[c % 2][:], scalar=0.5, in1=pu[:], op0=ALU.mult, op1=ALU.mult)
                for s in range(4):
                    pb0, pb1 = B[4 + (s % 2) * 2], B[5 + (s % 2) * 2]
                    for hf, pb in ((0, pb0), (1, pb1)):
                        for c in range(NFF):
                            fw.matmul(pb[:], h1T[:, c, s * 128:(s + 1) * 128], wd[:, c, hf * 512:(hf + 1) * 512],
                                      start=(c == 0), stop=(c == NFF - 1))
                    self.post_resid(xt, s, pb0, pb1, 0.5)
                self.store_tile(xt, t * TT)
        fw.fence()

    def xattn_phase(self, l):
        cfg, fw, nc = self.cfg, self.fw, self.nc
        fw.fence()
        with ExitStack() as st:
            sb = lambda n, sh, dt=F32: st.enter_context(nc.sbuf_tensor(n + "__u%d" % self.uid(), list(sh), dt))
            self.alloc_common(st)
            wq = sb("wq", [128, 8, D], BF16)
            wo = sb("wo", [128, 8, D], BF16)
            wk = sb("wk", [128, 8, D], BF16)
            wv = sb("wv", [128, 8, D], BF16)
            mnw = sb("mnw", [128, D])
            mt = sb("mt", [128, 2, D])
            mT = sb("mT", [128, 8, MEM], BF16)
            kT = [sb("kT%d" % i, [128, 8, MEM], BF16) for i in range(cfg.NS)]
            vv = [sb("vv%d" % i, [128, 2, D], BF16) for i in range(cfg.NS)]
            qT = sb("qT", [128, 8, TT], BF16)
            mx = sb("mx", [128, 4])
            nmx = sb("nmx", [128, 4])
            sm = sb("sm", [128, 4])
            rsm = sb("rsm", [128, 4])
            pp = sb("pp", [128, 4, MEM], BF16)
            pT = sb("pT", [128, 8, 128], BF16)
            osb = sb("osb", [128, D], BF16)
            oT = sb("oT", [128, 8, 128], BF16)
            self.load_norm_w(l, 2)
            B = self.bank
            for w, n in ((wq, "xattn_q"), (wo, "xattn_o"), (wk, "xattn_k"), (wv, "xattn_v")):
                fw.dma("sync", w[:], self.wbf[n][l * D:(l + 1) * D, :].rearrange("(k p) n -> p k n", p=128))
            fw.dma("sync", mnw[:], self.mem_norm[l].partition_broadcast(128))
            for b in range(cfg.NS):
                fw.dma("sync", mt[:], self.mem[b * MEM:(b + 1) * MEM, :].rearrange("(s p) d -> p s d", p=128))
                for s in range(2):
                    self.A(out=self.junk[:], in_=mt[:, s, :], func=AF.Square, accum_out=self.ss[:, s:s + 1])
                self.rstd_of(self.ss, self.rs, 2, 1.0 / D)
                for s in range(2):
                    ub = self.ub[s % 2]
                    self.V("scalar_tensor_tensor", out=ub[:], in0=mt[:, s, :], scalar=self.rs[:, s:s + 1],
                           in1=mnw[:], op0=ALU.mult, op1=ALU.mult)
                    tb = B[6 + s % 2].bitcast(BF16)
                    for j in range(8):
                        fw.transpose(tb[:, j * 128:(j + 1) * 128], ub[:, j * 128:(j + 1) * 128], self.identb[:])
                    self.A(out=mT[:, :, s * 128:(s + 1) * 128], in_=tb[:, :].rearrange("p (j t) -> p j t", j=8), func=AF.Copy)
                for fc in range(8):
                    pk = B[fc % 2]
                    for k in range(8):
                        fw.matmul(pk[:, 0:MEM], wk[:, k, fc * 128:(fc + 1) * 128], mT[:, k, :], start=(k == 0), stop=(k == 7))
                    self.V("tensor_copy", out=kT[b][:, fc, :], in_=pk[:, 0:MEM])
                for j in range(2):
                    for hf in range(2):
                        pv = B[2 + hf]
                        for k in range(8):
                            fw.matmul(pv[:], mT[:, k, j * 128:(j + 1) * 128], wv[:, k, hf * 512:(hf + 1) * 512], start=(k == 0), stop=(k == 7))
                        self.A(out=vv[b][:, j, hf * 512:(hf + 1) * 512], in_=pv[:], func=AF.Copy)
            import os
            XSTOP = int(os.environ.get("XSTOP", "9"))
            nxt = self.load_tile(0)
            for t in range(cfg.NT // TT if XSTOP > 0 else 0):
                b = (t * TT) // cfg.S
                xt = nxt
                self.norm_uT(xt, [B[6], B[7]])
                if t + 1 < cfg.NT // TT:
                    nxt = self.load_tile((t + 1) * TT)
                for fc in range(8):
                    pq = B[fc % 2]
                    for k in range(8):
                        fw.matmul(pq[:], wq[:, k, fc * 128:(fc + 1) * 128], self.uT[:, k, :], start=(k == 0), stop=(k == 7))
                    self.A(out=qT[:, fc, :], in_=pq[:], func=AF.Identity, scale=1.0 / 16.0)
                for s in range(4 if XSTOP > 1 else 0):
                    for h in range(4):
                        ps = B[2 + h // 2][:, (h % 2) * MEM:(h % 2 + 1) * MEM]
                        for j in range(2):
                            fw.matmul(ps, qT[:, 2 * h + j, s * 128:(s + 1) * 128], kT[b][:, 2 * h + j, :], start=(j == 0), stop=(j == 1))
                    for h2 in range(2):
                        self.V("tensor_reduce", out=mx[:, 2 * h2:2 * h2 + 2], in_=B[2 + h2][:, :].rearrange("p (h m) -> p h m", h=2),
                               axis=AX.X, op=ALU.max)
                    self.V("tensor_scalar", out=nmx[:], in0=mx[:], scalar1=-1.0, scalar2=None, op0=ALU.mult)
                    for h in range(4):
                        ps = B[2 + h // 2][:, (h % 2) * MEM:(h % 2 + 1) * MEM]
                        self.A(out=pp[:, h, :], in_=ps, func=AF.Exp, bias=nmx[:, h:h + 1], scale=1.0, accum_out=sm[:, h:h + 1])
                    self.V("reciprocal", out=rsm[:], in_=sm[:])
                    if XSTOP <= 2:
                        continue
                    tb = B[6 + s % 2].bitcast(BF16)
                    for h in range(4):
                        for j in range(2):
                            fw.transpose(tb[:, (2 * h + j) * 128:(2 * h + j + 1) * 128], pp[:, h, j * 128:(j + 1) * 128], self.identb[:])
                    self.A(out=pT[:], in_=tb[:, :].rearrange("p (j t) -> p j t", j=8), func=AF.Copy)
                    for h in range(4):
                        po = B[h // 2][:, (h % 2) * 256:(h % 2 + 1) * 256]
                        for j in range(2):
                            fw.matmul(po, pT[:, 2 * h + j, :], vv[b][:, j, h * 256:(h + 1) * 256], start=(j == 0), stop=(j == 1))
                    for h in range(4):
                        po = B[h // 2][:, (h % 2) * 256:(h % 2 + 1) * 256]
                        self.V("tensor_scalar", out=osb[:, h * 256:(h + 1) * 256], in0=po, scalar1=rsm[:, h:h + 1], scalar2=None, op0=ALU.mult)
                    if XSTOP <= 3:
                        continue
                    tb2 = B[4 + s % 2].bitcast(BF16)
                    for j in range(8):
                        fw.transpose(tb2[:, j * 128:(j + 1) * 128], osb[:, j * 128:(j + 1) * 128], self.identb[:])
                    self.A(out=oT[:], in_=tb2[:, :].rearrange("p (j t) -> p j t", j=8), func=AF.Copy)
                    pb0, pb1 = B[0], B[1]
                    for hf, pb in ((0, pb0), (1, pb1)):
                        for k in range(8):
                            fw.matmul(pb[:], oT[:, k, :], wo[:, k, hf * 512:(hf + 1) * 512], start=(k == 0), stop=(k == 7))
                    self.post_resid(xt, s, pb0, pb1, 1.0)
                self.store_tile(xt, t * TT)
        fw.fence()

    def build(self):
        cfg, fw = self.cfg, self.fw
        fw.dma("sync", self.y[:, :], self.x_in[:, :], semkey="xcopy")
        if "mix" in cfg.phases:
            self.rot_tables()
        for l in range(cfg.DEPTH):
            self.cast_weights(l)
        for l in range(cfg.DEPTH):
            for ph in cfg.phases:
                if ph == "ffn1":
                    self.ffn_phase(l, 1)
                elif ph == "ffn2":
                    self.ffn_phase(l, 2)
                elif ph == "xattn":
                    self.xattn_phase(l)
                elif ph == "mix":
                    self.mix_phase(l)
        with self.nc.allow_non_contiguous_dma(reason="small strided parameter loads"):
            stats = fw.finalize()
        return stats


PI = math.pi


def bc_mid(ap2d, n):
    return ap2d.unsqueeze(1).broadcast_to([ap2d.shape[0], n, ap2d.shape[1]])


def bc_last(ap2d, n):
    return ap2d.unsqueeze(2).broadcast_to([ap2d.shape[0], ap2d.shape[1], n])


def mix_consts(self):
    fw = self.fw
    sbt = fw.sb
    self.NM_lo = sbt("NM_lo", [128, 128])
    self.NM_up = sbt("NM_up", [128, 128])
    self.OFFD = sbt("OFFD", [128, 128])
    self.UP1 = sbt("UP1", [128, 128])
    self.LO1 = sbt("LO1", [128, 128])
    self.PTb = sbt("PTb", [128, 128], BF16)
    self.onesb = sbt("onesb", [128, 128], BF16)
    self.POS = sbt("POS", [128, 128])
    self.NEG = sbt("NEG", [128, 128])
    self.ROW1 = sbt("ROW1", [128, 128])
    self.ROWB = sbt("ROWB", [128, 128])
    self.COLF = sbt("COLF", [128, 1])
    self.COLB = sbt("COLB", [128, 1])
    self.invf = sbt("invf", [128, 1])
    self.qsc = sbt("qsc", [128, 1])
    itmp = sbt("itmp", [128, 128], I32)
    ftmp = sbt("ftmp", [128, 128])
    G, V, A = self.G, self.V, self.A
    G("affine_select", out=self.NM_lo[:], in_=self.zerosf[:], pattern=[[-1, 128]], compare_op=ALU.is_ge, fill=-1e30, base=0, channel_multiplier=1)
    G("affine_select", out=self.NM_up[:], in_=self.zerosf[:], pattern=[[1, 128]], compare_op=ALU.is_ge, fill=-1e30, base=0, channel_multiplier=-1)
    G("affine_select", out=self.LO1[:], in_=self.onesf[:], pattern=[[-1, 128]], compare_op=ALU.is_ge, fill=0.0, base=0, channel_multiplier=1)
    G("affine_select", out=self.UP1[:], in_=self.onesf[:], pattern=[[1, 128]], compare_op=ALU.is_ge, fill=0.0, base=0, channel_multiplier=-1)
    G("affine_select", out=self.OFFD[:], in_=self.onesf[:], pattern=[[1, 128]], compare_op=ALU.not_equal, fill=0.0, base=0, channel_multiplier=-1)
    G("affine_select", out=ftmp[:], in_=self.onesf[:], pattern=[[1, 128]], compare_op=ALU.is_equal, fill=0.0, base=-64, channel_multiplier=-1)
    G("affine_select", out=self.POS[:], in_=self.onesf[:], pattern=[[1, 128]], compare_op=ALU.is_equal, fill=0.0, base=64, channel_multiplier=-1)
    V("tensor_tensor", out=self.PTb[:], in0=ftmp[:], in1=self.POS[:], op=ALU.subtract)
    V("tensor_copy", out=self.onesb[:], in_=self.onesf[:])
    G("iota", out=itmp[:], pattern=[[1, 128]], base=0, channel_multiplier=-1)
    V("tensor_copy", out=ftmp[:], in_=itmp[:])
    V("tensor_scalar", out=self.POS[:], in0=ftmp[:], scalar1=0.0, scalar2=None, op0=ALU.max)
    V("tensor_scalar", out=self.NEG[:], in0=ftmp[:], scalar1=-1.0, scalar2=0.0, op0=ALU.mult, op1=ALU.max)
    G("iota", out=itmp[:], pattern=[[1, 128]], base=1, channel_multiplier=0)
    V("tensor_copy", out=self.ROW1[:], in_=itmp[:])
    G("iota", out=itmp[:], pattern=[[-1, 128]], base=128, channel_multiplier=0)
    V("tensor_copy", out=self.ROWB[:], in_=itmp[:])
    G("iota", out=itmp[:, 0:1], pattern=[[0, 1]], base=127, channel_multiplier=-1)
    V("tensor_copy", out=self.COLF[:], in_=itmp[:, 0:1])
    G("iota", out=itmp[:, 1:2], pattern=[[0, 1]], base=0, channel_multiplier=1)
    V("tensor_copy", out=self.COLB[:], in_=itmp[:, 1:2])
    G("iota", out=itmp[0:64, 2:3], pattern=[[0, 1]], base=0, channel_multiplier=1)
    G("iota", out=itmp[64:128, 2:3], pattern=[[0, 1]], base=0, channel_multiplier=1)
    V("tensor_copy", out=ftmp[:, 0:1], in_=itmp[:, 2:3])
    A(out=self.invf[:], in_=ftmp[:, 0:1], func=AF.Exp, scale=-math.log(10000.0) / 64.0)
    G("memset", ap=self.qsc[:], constant=-0.5 * math.log(128.0))
    self.BD16 = sbt("BD16", [128, 128]); self.MO32 = sbt("MO32", [128, 128]); self.MO64 = sbt("MO64", [128, 128]); self.MO128 = sbt("MO128", [128, 128])
    itf = sbt("itf", [128, 128], I32); itp = sbt("itp", [128, 128], I32); its = sbt("its", [128, 128], I32)
    ff = sbt("ff", [128, 128]); fp_ = sbt("fp_", [128, 128]); bd = [sbt("bd%d" % i, [128, 128]) for i in range(3)]
    G("iota", out=itf[:], pattern=[[1, 128]], base=0, channel_multiplier=0)
    G("iota", out=itp[:], pattern=[[0, 128]], base=0, channel_multiplier=1)
    for i, sh in enumerate((4, 5, 6)):
        V("tensor_scalar", out=its[:], in0=itf[:], scalar1=sh, scalar2=None, op0=ALU.arith_shift_right)
        V("tensor_copy", out=ff[:], in_=its[:])
        V("tensor_scalar", out=its[:], in0=itp[:], scalar1=sh, scalar2=None, op0=ALU.arith_shift_right)
        V("tensor_copy", out=fp_[:], in_=its[:])
        V("tensor_tensor", out=bd[i][:], in0=ff[:], in1=fp_[:], op=ALU.is_equal)
    V("tensor_copy", out=self.BD16[:], in_=bd[0][:])
    V("tensor_tensor", out=self.MO32[:], in0=bd[1][:], in1=bd[0][:], op=ALU.subtract)
    V("tensor_tensor", out=self.MO64[:], in0=bd[2][:], in1=bd[1][:], op=ALU.subtract)
    V("tensor_scalar", out=self.MO128[:], in0=bd[2][:], scalar1=-1.0, scalar2=1.0, op0=ALU.mult, op1=ALU.add)


def rot_tables(self):
    cfg, fw, nc = self.cfg, self.fw, self.nc
    S = cfg.S
    V, A, G = self.V, self.A, self.G
    self.ROT = nc.dram_tensor("ROT", [cfg.NS, 2, 128, S], F32, kind="Internal").ap()
    import os
    if os.environ.get("ROTSKIP"):
        return
    with ExitStack() as st:
        sb = lambda n, sh, dt=F32: st.enter_context(nc.sbuf_tensor(n + "__u%d" % self.uid(), list(sh), dt))
        posi = sb("posi", [128, TT], I32)
        ang = sb("ang", [128, TT]); kq = sb("kq", [128, TT]); kqi = sb("kqi", [128, TT], I32)
        sinT = sb("sinT", [128, TT]); cosT = sb("cosT", [128, TT])
        for b in range(cfg.NS):
            for ti in range(S // TT):
                t0 = ti * TT
                fw.dma("sync", posi[:], self.pos[b * S + t0:b * S + t0 + TT].partition_broadcast(128))
                V("tensor_copy", out=ang[:], in_=posi[:])
                V("tensor_scalar", out=ang[:], in0=ang[:], scalar1=self.invf[:, 0:1], scalar2=None, op0=ALU.mult)
                V("tensor_scalar", out=kq[:], in0=ang[:], scalar1=1.0 / (2 * PI), scalar2=None, op0=ALU.mult)
                V("tensor_copy", out=kqi[:], in_=kq[:])
                V("tensor_copy", out=kq[:], in_=kqi[:])
                V("scalar_tensor_tensor", out=ang[:], in0=kq[:], scalar=-2 * PI, in1=ang[:], op0=ALU.mult, op1=ALU.add)
                for dst, shift in ((sinT, 0.0), (cosT, PI / 2)):
                    V("tensor_scalar", out=kq[:], in0=ang[:], scalar1=shift, scalar2=None, op0=ALU.add)
                    V("tensor_scalar", out=dst[:], in0=kq[:], scalar1=PI, scalar2=-2 * PI, op0=ALU.is_gt, op1=ALU.mult)
                    V("tensor_tensor", out=kq[:], in0=kq[:], in1=dst[:], op=ALU.add)
                    V("tensor_scalar", out=dst[:], in0=kq[:], scalar1=-PI, scalar2=2 * PI, op0=ALU.is_lt, op1=ALU.mult)
                    V("tensor_tensor", out=kq[:], in0=kq[:], in1=dst[:], op=ALU.add)
                    V("tensor_scalar", out=kq[:], in0=kq[:], scalar1=PI, scalar2=-PI, op0=ALU.min, op1=ALU.max)
                    A(out=dst[:], in_=kq[:], func=AF.Sin)
                fw.dma("sync", self.ROT[b, 0, :, t0:t0 + TT], sinT[:], semkey="st_sin")
                fw.dma("sync", self.ROT[b, 1, :, t0:t0 + TT], cosT[:], semkey="st_cos")
    fw.fence()


def mix_phase(self, l):
    cfg, fw, nc = self.cfg, self.fw, self.nc
    S = cfg.S
    NCH = S // 128
    B = self.bank
    V, A, G = self.V, self.A, self.G
    KS = 128.0 ** -0.5
    if not hasattr(self, "RQ"):
        dr = lambda n, sh, dt=BF16: nc.dram_tensor(n, list(sh), dt, kind="Internal").ap()
        self.RQ = dr("RQ", [NCH, 128, 4, 128]); self.RK = dr("RK", [NCH, 128, 4, 128])
        self.RKt = dr("RKt", [NCH, 128, 4, 128]); self.RV = dr("RV", [NCH, 128, 512]); self.RGt = dr("RGt", [NCH, 128, 512])
        self.PRE = dr("PRE", [12, 128, S + 4], F32)
        self.GQ = dr("GQ", [NCH, 128, 4, 128]); self.GK = dr("GK", [NCH, 128, 4, 128])
        self.GKt = dr("GKt", [NCH, 128, 4, 128]); self.GV = dr("GV", [NCH, 128, 4, 128]); self.GZ = dr("GZ", [NCH, 128, 512])
        self.GB = dr("GB", [NCH, 128, 16], F32)
        self.OB = dr("OB", [NCH, 128, 1024], F32)
    win_bf = self.wbf["w_in"][l * D:(l + 1) * D, :].rearrange("(k p) n -> p k n", p=128)
    wout_bf = self.wbf["w_out"][l * D:(l + 1) * D, :].rearrange("(k p) n -> p k n", p=128)

    import os
    for b in range(cfg.NS if int(os.environ.get("MSTOP", "9")) > 0 else 0):
        fw.fence()
        with ExitStack() as st:
            sb = lambda n, sh, dt=F32: st.enter_context(nc.sbuf_tensor(n + "__u%d" % self.uid(), list(sh), dt))
            self.alloc_common(st, nxt=1, post=False)
            win = sb("win", [128, 8, INC], BF16)
            sinT = sb("sinT", [128, TT]); cosT = sb("cosT", [128, TT])
            gtmp = [sb("gtmp%d" % i, [128, TT]) for i in range(2)]
            xb = [sb("xb%d" % i, [128, TT], BF16) for i in range(2)]
            t1 = [sb("t1_%d" % i, [128, TT]) for i in range(2)]
            t2 = [sb("t2_%d" % i, [128, TT]) for i in range(2)]
            rq_t = sb("rq_t", [128, 4, 4, 128], BF16); rk_t = sb("rk_t", [128, 4, 4, 128], BF16)
            rkt_t = sb("rkt_t", [128, 4, 4, 128], BF16)
            rv_t = sb("rv_t", [128, 4, 512], BF16); rg_t = sb("rg_t", [128, 4, 512], BF16); gz_t = sb("gz_t", [128, 4, 512], BF16)
            gb_t = sb("gb_t", [128, 4, 16])
            pre_s = [sb("pre_s%d" % i, [128, TT]) for i in range(2)]
            zt = sb("zt", [128, 2])
            negA = sb("negA", [128, 8]); dtb = sb("dtb", [128, 8])
            sp = sb("sp", [128, 5, 8])
            self.load_norm_w(l, 1)
            fw.dma("sync", win[:, 0:4, :], win_bf[:, 0:4, :])
            fw.dma("sync", win[:, 4:8, :], win_bf[:, 4:8, :], semkey="win_b")
            fw.dma("sync", negA[:], self.a_log[l].partition_broadcast(128))
            fw.dma("sync", dtb[:], self.dt_bias[l].partition_broadcast(128))
            A(out=negA[:], in_=negA[:], func=AF.Exp)
            V("tensor_scalar", out=negA[:], in0=negA[:], scalar1=-1.0, scalar2=None, op0=ALU.mult)
            G("memset", ap=zt[:], constant=0.0)
            M1S = os.environ.get("M1S", "")
            for c in range(12 if "a" not in M1S else 0):
                fw.dma("sync", self.PRE[c, :, 0:2], zt[:, 0:2], semkey="prez")
                fw.dma("sync", self.PRE[c, :, S + 2:S + 4], zt[:, 0:2], semkey="prez")
            nxt = self.load_tile(b * S)
            for ti in range(S // TT):
                t0 = ti * TT
                xt = nxt
                self.norm_uT(xt, [B[6], B[7]])
                if ti + 1 < S // TT:
                    nxt = self.load_tile(b * S + t0 + TT)
                fw.dma("sync", sinT[:], self.ROT[b, 0, :, t0:t0 + TT], semkey="l_sin")
                fw.dma("sync", cosT[:], self.ROT[b, 1, :, t0:t0 + TT], semkey="l_cos")
                for fc in range(20 if "c" not in M1S else 0):
                    col0 = fc * 128 if fc < 8 else 2048 + (fc - 8) * 128
                    pb = B[fc % 2]
                    for k in range(8):
                        fw.matmul(pb[:], win[:, k, col0:col0 + 128], self.uT[:, k, :], start=(k == 0), stop=(k == 7))
                    if fc < 8 and "d" in M1S:
                        A(out=xb[fc % 2][:], in_=pb[:], func=AF.Copy)
                    elif fc < 8:
                        h = fc % 4
                        sc = 1.0 if fc < 4 else KS
                        x_b = xb[fc % 2]; a1 = t1[fc % 2]; a2 = t2[fc % 2]
                        V("scalar_tensor_tensor", out=a1[:], in0=pb[:], scalar=sc, in1=cosT[:], op0=ALU.mult, op1=ALU.mult)
                        A(out=x_b[:], in_=pb[:], func=AF.Copy, ins=[a1[:]])
                        px = B[2 + fc % 2]
                        fw.matmul(px[:], self.PTb[:], x_b[:])
                        V("scalar_tensor_tensor", out=a2[:], in0=px[:], scalar=sc, in1=sinT[:], op0=ALU.mult, op1=ALU.mult)
                        dst = rq_t if fc < 4 else rk_t
                        if "f" in M1S:
                            V("tensor_tensor", out=a1[:], in0=a1[:], in1=a2[:], op=ALU.add)
                        else:
                            V("tensor_tensor", out=dst[:, :, h, :], in0=a1[:, :].rearrange("p (s t) -> p s t", s=4),
                              in1=a2[:, :].rearrange("p (s t) -> p s t", s=4), op=ALU.add)
                        if fc >= 4 and "g" not in M1S:
                            tb = B[4 + fc % 2].bitcast(BF16)
                            for s in range(4):
                                fw.transpose(tb[:, s * 128:(s + 1) * 128], rk_t[:, s, h, :], self.identb[:])
                            A(out=rkt_t[:, :, h, :], in_=tb[:, 0:512].rearrange("p (s t) -> p s t", s=4), func=AF.Copy)
                    else:
                        c = fc - 8
                        ps_ = pre_s[fc % 2]
                        A(out=ps_[:], in_=pb[:], func=AF.Copy)
                        if "e" not in M1S:
                            fw.dma("sync", self.PRE[c, :, 2 + t0:2 + t0 + TT], ps_[:], semkey=("pre", fc % 2))
                n0 = ti * 4
                fw.dma("sync", self.RQ[n0:n0 + 4].rearrange("n p h t -> p n h t"), rq_t[:], semkey="st_rq")
                fw.dma("sync", self.RK[n0:n0 + 4].rearrange("n p h t -> p n h t"), rk_t[:], semkey="st_rk")
                fw.dma("sync", self.RKt[n0:n0 + 4].rearrange("n p h t -> p n h t"), rkt_t[:], semkey="st_rkt")
                for s in range(4 if "b" not in M1S else 0):
                    for kind, col0 in (("v", 1024), ("g", 1536), ("z", 3584)):
                        pb = B[{"v": 2, "g": 3, "z": 4}[kind]]
                        for k in range(8):
                            fw.matmul(pb[:], self.uT[:, k, s * 128:(s + 1) * 128], win[:, k, col0:col0 + 512], start=(k == 0), stop=(k == 7))
                        if kind == "v":
                            V("tensor_copy", out=rv_t[:, s, :], in_=pb[:])
                        else:
                            gt = gtmp[0 if kind == "g" else 1]
                            A(out=gt[:], in_=pb[:], func=AF.Tanh, scale=0.5)
                            V("scalar_tensor_tensor", out=gt[:], in0=gt[:], scalar=1.0, in1=pb[:], op0=ALU.add, op1=ALU.mult)
                            V("tensor_scalar", out=(rg_t if kind == "g" else gz_t)[:, s, :], in0=gt[:], scalar1=0.5, scalar2=None, op0=ALU.mult)
                    pb = B[5]
                    for k in range(8):
                        fw.matmul(pb[:, 0:16], self.uT[:, k, s * 128:(s + 1) * 128], win[:, k, 4096:4112], start=(k == 0), stop=(k == 7))
                    V("tensor_tensor", out=sp[:, 0, :], in0=pb[:, 0:8], in1=dtb[:], op=ALU.add)
                    V("scalar_tensor_tensor", out=sp[:, 1, :], in0=sp[:, 0, :], scalar=-1.0, in1=sp[:, 0, :], op0=ALU.mult, op1=ALU.min)
                    A(out=sp[:, 2, :], in_=sp[:, 1, :], func=AF.Exp, scale=1.0)
                    V("tensor_scalar", out=sp[:, 3, :], in0=sp[:, 2, :], scalar1=2.0, scalar2=None, op0=ALU.add)
                    V("reciprocal", out=sp[:, 3, :], in_=sp[:, 3, :])
                    V("tensor_tensor", out=sp[:, 1, :], in0=sp[:, 2, :], in1=sp[:, 3, :], op=ALU.mult)
                    V("tensor_tensor", out=sp[:, 2, :], in0=sp[:, 1, :], in1=sp[:, 1, :], op=ALU.mult)
                    V("tensor_scalar", out=sp[:, 3, :], in0=sp[:, 2, :], scalar1=1.0 / 13.0, scalar2=1.0 / 11.0, op0=ALU.mult, op1=ALU.add)
                    for cf in (1.0 / 9.0, 1.0 / 7.0, 1.0 / 5.0, 1.0 / 3.0, 1.0):
                        V("tensor_tensor", out=sp[:, 3, :], in0=sp[:, 3, :], in1=sp[:, 2, :], op=ALU.mult)
                        V("tensor_scalar", out=sp[:, 3, :], in0=sp[:, 3, :], scalar1=cf, scalar2=None, op0=ALU.add)
                    V("tensor_tensor", out=sp[:, 3, :], in0=sp[:, 3, :], in1=sp[:, 1, :], op=ALU.mult)
                    V("tensor_scalar", out=sp[:, 4, :], in0=sp[:, 0, :], scalar1=0.0, scalar2=None, op0=ALU.max)
                    V("scalar_tensor_tensor", out=sp[:, 4, :], in0=sp[:, 3, :], scalar=2.0, in1=sp[:, 4, :], op0=ALU.mult, op1=ALU.add)
                    V("tensor_tensor", out=gb_t[:, s, 0:8], in0=sp[:, 4, :], in1=negA[:], op=ALU.mult)
                    A(out=sp[:, 1, :], in_=pb[:, 8:16], func=AF.Tanh, scale=0.5, ins=[gb_t[:, s, 0:8]])
                    V("tensor_scalar", out=gb_t[:, s, 8:16], in0=sp[:, 1, :], scalar1=0.5, scalar2=0.5, op0=ALU.mult, op1=ALU.add)
                fw.dma("sync", self.RV[n0:n0 + 4].rearrange("n p f -> p n f"), rv_t[:], semkey="st_rv")
                fw.dma("sync", self.RGt[n0:n0 + 4].rearrange("n p f -> p n f"), rg_t[:], semkey="st_rg")
                fw.dma("sync", self.GZ[n0:n0 + 4].rearrange("n p f -> p n f"), gz_t[:], semkey="st_gz")
                fw.dma("sync", self.GB[n0:n0 + 4].rearrange("n p f -> p n f"), gb_t[:], semkey="st_gb")
        import os
        MSTOP = int(os.environ.get("MSTOP", "9"))
        if MSTOP <= 1:
            continue
        fw.fence()
        with ExitStack() as st:
            sb = lambda n, sh, dt=F32: st.enter_context(nc.sbuf_tensor(n + "__u%d" % self.uid(), list(sh), dt))
            cw = sb("cw", [128, 12, 5])
            pc = [sb("pc%d" % i, [128, TT + 4]) for i in range(2)]
            acc = [sb("acc%d" % i, [128, TT]) for i in range(2)]
            sil = [sb("sil%d" % i, [128, TT]) for i in range(2)]
            sq = [sb("sq%d" % i, [128, TT], BF16) for i in range(2)]
            rn = [sb("rn%d" % i, [128, TT]) for i in range(2)]
            vb_ = [sb("vb_%d" % i, [128, TT], BF16) for i in range(2)]
            gq_t = sb("gq_t", [128, 4, 4, 128], BF16); gk_t = sb("gk_t", [128, 4, 4, 128], BF16)
            gkt_t = sb("gkt_t", [128, 4, 4, 128], BF16); gv_t = sb("gv_t", [128, 4, 4, 128], BF16)
            for k_ in range(5):
                fw.dma("sync", cw[:, :, k_], self.gdn_conv[l * 5 + k_, :].rearrange("(c p) -> p c", p=128), semkey="cw")
            ci = 0
            for ti in range(S // TT):
                t0 = ti * TT
                for c in range(12):
                    h = c % 4
                    i2 = ci % 2
                    ci += 1
                    fw.dma("sync", pc[i2][:], self.PRE[c, :, t0:t0 + TT + 4], semkey=("pc", i2))
                    V("tensor_scalar", out=acc[i2][:], in0=pc[i2][:, 0:TT], scalar1=cw[:, c, 0:1], scalar2=None, op0=ALU.mult)
                    for j in range(1, 5):
                        V("scalar_tensor_tensor", out=acc[i2][:], in0=pc[i2][:, j:j + TT], scalar=cw[:, c, j:j + 1], in1=acc[i2][:],
                          op0=ALU.mult, op1=ALU.add)
                    A(out=sil[i2][:], in_=acc[i2][:], func=AF.Tanh, scale=0.5)
                    V("scalar_tensor_tensor", out=sil[i2][:], in0=sil[i2][:], scalar=1.0, in1=acc[i2][:], op0=ALU.add, op1=ALU.mult)
                    if c < 8:
                        A(out=sq[i2][:], in_=sil[i2][:], func=AF.Square)
                        pn = B[c % 2]
                        fw.matmul(pn[:], self.onesb[:], sq[i2][:])
                        V("tensor_scalar", out=rn[i2][:], in0=pn[:], scalar1=4.0 * 1e-6, scalar2=None, op0=ALU.add)
                        G("tensor_tensor", out=rn[i2][:], in0=rn[i2][:], in1=self.mhalf[:, :], op=ALU.pow)
                        if c < 4:
                            V("tensor_scalar", out=sil[i2][:], in0=sil[i2][:], scalar1=KS, scalar2=None, op0=ALU.mult)
                        dst = gq_t if c < 4 else gk_t
                        V("tensor_tensor", out=dst[:, :, h, :], in0=sil[i2][:, :].rearrange("p (s t) -> p s t", s=4),
                          in1=rn[i2][:, :].rearrange("p (s t) -> p s t", s=4), op=ALU.mult)
                        if c >= 4:
                            tb = B[2 + c % 2].bitcast(BF16)
                            for s in range(4):
                                fw.transpose(tb[:, s * 128:(s + 1) * 128], gk_t[:, s, h, :], self.identb[:])
                            A(out=gkt_t[:, :, h, :], in_=tb[:, 0:512].rearrange("p (s t) -> p s t", s=4), func=AF.Copy)
                    else:
                        V("tensor_scalar", out=vb_[i2][:], in0=sil[i2][:], scalar1=0.5, scalar2=None, op0=ALU.mult)
                        tb = B[4 + c % 2].bitcast(BF16)
                        for s in range(4):
                            fw.transpose(tb[:, s * 128:(s + 1) * 128], vb_[i2][:, s * 128:(s + 1) * 128], self.identb[:])
                        V("tensor_copy", out=gv_t[:, :, h, :], in_=tb[:, 0:512].rearrange("p (s t) -> p s t", s=4))
                n0 = ti * 4
                fw.dma("sync", self.GQ[n0:n0 + 4].rearrange("n p h t -> p n h t"), gq_t[:], semkey="st_gq")
                fw.dma("sync", self.GK[n0:n0 + 4].rearrange("n p h t -> p n h t"), gk_t[:], semkey="st_gk")
                fw.dma("sync", self.GKt[n0:n0 + 4].rearrange("n p h t -> p n h t"), gkt_t[:], semkey="st_gkt")
                fw.dma("sync", self.GV[n0:n0 + 4].rearrange("n p h t -> p n h t"), gv_t[:], semkey="st_gv")
        if MSTOP <= 2:
            continue
        fw.fence()
        with ExitStack() as st:
            sb = lambda n, sh, dt=F32: st.enter_context(nc.sbuf_tensor(n + "__u%d" % self.uid(), list(sh), dt))
            T = type("T", (), {})()
            lg = sb("lg", [128, 8]); cd = sb("cd", [128, 8]); kdc = sb("kdc", [128, 8])
            DT = sb("DT", [128, 4, 128]); QDF = sb("QDF", [128, 4, 128]); QDB = sb("QDB", [128, 4, 128])
            e1 = sb("e1", [128, 128]); e2 = sb("e2", [128, 128])
            gnw = sb("gnw", [128, 128])
            wout = sb("wout", [128, 8, D], BF16)
            wpost = sb("wpost", [128, D])
            fw.dma("sync", lg[:], self.ret_lg[l].partition_broadcast(128))
            fw.dma("sync", gnw[:], self.gdn_norm[l].partition_broadcast(128))
            fw.dma("sync", wout[:], wout_bf)
            fw.dma("sync", wpost[:], self.norm_post[l * 4 + 1].partition_broadcast(128))
            self.wpost = wpost
            A(out=cd[:], in_=lg[:], func=AF.Exp, scale=128.0)
            for h in range(4):
                V("tensor_scalar", out=e1[:], in0=self.POS[:], scalar1=lg[:, h:h + 1], scalar2=None, op0=ALU.mult)
                V("scalar_tensor_tensor", out=e2[:], in0=self.NEG[:], scalar=lg[:, 4 + h:5 + h], in1=e1[:], op0=ALU.mult, op1=ALU.add)
                A(out=DT[:, h, :], in_=e2[:], func=AF.Exp)
                A(out=QDF[:, h, :], in_=self.ROW1[:], func=AF.Exp, scale=lg[:, h:h + 1])
                A(out=QDB[:, h, :], in_=self.ROWB[:], func=AF.Exp, scale=lg[:, 4 + h:5 + h])
                A(out=kdc[:, h:h + 1], in_=self.COLF[:], func=AF.Exp, scale=lg[:, h:h + 1])
                A(out=kdc[:, 4 + h:5 + h], in_=self.COLB[:], func=AF.Exp, scale=lg[:, 4 + h:5 + h])
            bft = lambda n: sb(n, [128, 4, 128], BF16)
            f32t = lambda n: sb(n, [128, 4, 128])
            rq = [bft("rq%d" % i) for i in range(2)]; rk = [bft("rk%d" % i) for i in range(2)]
            rkt = [bft("rkt%d" % i) for i in range(2)]; rv = [bft("rv%d" % i) for i in range(2)]
            rg = [sb("rg%d" % i, [128, 512], BF16) for i in range(2)]
            gq = [bft("gq%d" % i) for i in range(2)]; gk = [bft("gk%d" % i) for i in range(2)]
            gkt = [bft("gkt%d" % i) for i in range(2)]; gv = [bft("gv%d" % i) for i in range(2)]
            gz = [sb("gz%d" % i, [128, 512], BF16) for i in range(2)]
            gbt = [sb("gbt%d" % i, [128, 16]) for i in range(2)]
            obt = [sb("obt%d" % i, [128, 1024]) for i in range(2)]
            xc = [sb("xc%d" % i, [128, 1, D]) for i in range(2)]
            Sr = f32t("Sr"); Srb = bft("Srb"); Sg = f32t("Sg"); Sgb = bft("Sgb")
            qd = bft("qd"); kd = bft("kd"); sT = bft("sT")
            T.Gc = sb("Gc", [128, 4]); T.nGc = sb("nGc", [128, 4]); T.nbeta = sb("nbeta", [128, 4])
            T.dg = f32t("dg"); T.db = f32t("db"); T.eG = f32t("eG"); T.tE = f32t("tE"); T.tET = f32t("tET")
            T.E = f32t("E"); T.ET = f32t("ET"); T.Es = f32t("Es"); T.ETs = f32t("ETs"); T.tq = f32t("tq")
            T.kg = bft("kg"); T.qg = bft("qg"); T.QKET = bft("QKET")
            T.R = [bft("R0")]
            T.P0 = f32t("P0"); T.Q0 = f32t("Q0")
            T.Pd = [f32t("Pd%d" % i) for i in range(2)]; T.Qd = [f32t("Qd%d" % i) for i in range(2)]
            T.X = [f32t("X%d" % i) for i in range(2)]; T.XT = [f32t("XT%d" % i) for i in range(2)]
            T.Y = f32t("Yg"); T.YT = f32t("YTg"); T.O = f32t("Og"); T.OT = f32t("OTg")
            T.rr = bft("rr"); T.vn = bft("vn"); T.kt = bft("kt")
            otot = sb("otot", [128, 1024]); og = sb("og", [128, 1024], BF16); ogT = sb("ogT", [128, 8, 128], BF16)
            st6 = sb("st6", [128, 4, 6]); mv = sb("mv", [128, 4, 2]); rstd = sb("rstd", [128, 8]); tmpn = sb("tmpn", [128, 512])
            ssg = sb("ssg", [128, 4])
            self.ss2 = sb("ss2m", [128, 4]); self.rs2 = sb("rs2m", [128, 4])
            self.tpost = [sb("tpostm%d" % i, [128, D]) for i in range(2)]

            def gdn_chunk(r, gq_, gk_, gkt_, gv_, gb_, S32, Sbf, obank):
                g = gb_[:, r * 4:(r + 1) * 4]
                beta = gb_[:, 8 + r * 4:8 + (r + 1) * 4]
                tri = self.UP1 if r == 0 else self.LO1
                NMi = self.NM_lo if r == 0 else self.NM_up
                NMj = self.NM_up if r == 0 else self.NM_lo
                last = 127 if r == 0 else 0
                v4 = lambda bank: bank[:, :].rearrange("p (h t) -> p h t", h=4)
                fw.matmul(B[2][:, 0:4], tri[:], g)
                V("tensor_copy", out=T.Gc[:], in_=B[2][:, 0:4])
                V("tensor_scalar", out=T.nGc[:], in0=T.Gc[:], scalar1=-1.0, scalar2=None, op0=ALU.mult)
                V("tensor_scalar", out=T.nbeta[:], in0=beta, scalar1=-1.0, scalar2=None, op0=ALU.mult)
                V("tensor_tensor", out=T.dg[:], in0=bc_mid(self.identf[:, :], 4), in1=bc_last(T.Gc[:, 0:4], 128), op=ALU.mult)
                V("tensor_tensor", out=T.db[:], in0=bc_mid(self.identf[:, :], 4), in1=bc_last(beta, 128), op=ALU.mult)
                fw.matmul(B[3][:], self.onesf[:], T.dg[:, :, :].rearrange("p h t -> p (h t)"))
                fw.matmul(B[4][:], self.onesf[:], T.db[:, :, :].rearrange("p h t -> p (h t)"))
                A(out=T.eG[:], in_=v4(B[3]), func=AF.Exp)
                V("tensor_tensor", out=T.kg[:], in0=gk_[:], in1=T.eG[:], op=ALU.mult)
                V("tensor_tensor", out=T.qg[:], in0=gq_[:], in1=T.eG[:], op=ALU.mult)
                V("scalar_tensor_tensor", out=T.tE[:], in0=v4(B[3]), scalar=-1.0, in1=bc_mid(NMi[:, :], 4), op0=ALU.mult, op1=ALU.add,
                  ins=[T.eG[:]])
                V("tensor_tensor", out=T.tET[:], in0=v4(B[3]), in1=bc_mid(NMj[:, :], 4), op=ALU.add)
                for h in range(4):
                    A(out=T.E[:, h, :], in_=T.tE[:, h, :], func=AF.Exp, bias=T.Gc[:, h:h + 1], scale=1.0)
                    A(out=T.ET[:, h, :], in_=T.tET[:, h, :], func=AF.Exp, bias=T.nGc[:, h:h + 1], scale=1.0)
                V("tensor_tensor", out=T.Es[:], in0=T.E[:], in1=bc_mid(self.OFFD[:, :], 4), op=ALU.mult)
                V("tensor_tensor", out=T.ETs[:], in0=T.ET[:], in1=bc_mid(self.OFFD[:, :], 4), op=ALU.mult)
                for h in range(4):
                    fw.matmul(B[2][:, h * 128:(h + 1) * 128], gk_[:, h, :], gk_[:, h, :])
                for h in range(4):
                    fw.matmul(B[1][:, h * 128:(h + 1) * 128], gk_[:, h, :], gq_[:, h, :])
                V("tensor_tensor", out=T.QKET[:], in0=v4(B[1]), in1=T.ET[:], op=ALU.mult)
                for h in range(4):
                    V("scalar_tensor_tensor", out=T.P0[:, h, :], in0=B[2][:, h * 128:(h + 1) * 128], scalar=T.nbeta[:, h:h + 1],
                      in1=T.ETs[:, h, :], op0=ALU.mult, op1=ALU.mult)
                V("tensor_tensor", out=T.tq[:], in0=T.Es[:], in1=v4(B[4]), op=ALU.mult)
                V("scalar_tensor_tensor", out=T.Q0[:], in0=v4(B[2]), scalar=-1.0, in1=T.tq[:], op0=ALU.mult, op1=ALU.mult)
                Pd, Qd, X, XT = T.Pd, T.Qd, T.X, T.XT
                V("tensor_tensor", out=Pd[0][:], in0=T.P0[:], in1=bc_mid(self.BD16[:, :], 4), op=ALU.mult)
                V("tensor_tensor", out=Qd[0][:], in0=T.Q0[:], in1=bc_mid(self.BD16[:, :], 4), op=ALU.mult)
                V("tensor_tensor", out=X[0][:], in0=Pd[0][:], in1=bc_mid(self.identf[:, :], 4), op=ALU.add)
                V("tensor_tensor", out=XT[0][:], in0=Qd[0][:], in1=bc_mid(self.identf[:, :], 4), op=ALU.add)
                xi = 0
                for k in range(1, 4):
                    a, bb = (k - 1) % 2, k % 2
                    for h in range(4):
                        fw.matmul(B[5][:, h * 128:(h + 1) * 128], Qd[a][:, h, :], Pd[a][:, h, :])
                    for h in range(4):
                        fw.matmul(B[6][:, h * 128:(h + 1) * 128], Pd[a][:, h, :], Qd[a][:, h, :])
                    A(out=Pd[bb][:], in_=v4(B[5]), func=AF.Copy)
                    V("tensor_copy", out=Qd[bb][:], in_=v4(B[6]))
                    for h in range(4):
                        fw.matmul(B[7][:, h * 128:(h + 1) * 128], Qd[bb][:, h, :], X[xi][:, h, :])
                    for h in range(4):
                        fw.matmul(B[1][:, h * 128:(h + 1) * 128], Pd[bb][:, h, :], XT[xi][:, h, :])
                    V("tensor_tensor", out=X[1 - xi][:], in0=X[xi][:], in1=v4(B[7]), op=ALU.add)
                    V("tensor_tensor", out=XT[1 - xi][:], in0=XT[xi][:], in1=v4(B[1]), op=ALU.add)
                    xi = 1 - xi
                for li, MO in enumerate((self.MO32, self.MO64, self.MO128)):
                    V("tensor_tensor", out=T.O[:], in0=T.P0[:], in1=bc_mid(MO[:, :], 4), op=ALU.mult)
                    V("tensor_tensor", out=T.OT[:], in0=T.Q0[:], in1=bc_mid(MO[:, :], 4), op=ALU.mult)
                    for h in range(4):
                        fw.matmul(B[5][:, h * 128:(h + 1) * 128], T.OT[:, h, :], X[xi][:, h, :])
                    A(out=T.Y[:], in_=v4(B[5]), func=AF.Copy)
                    for h in range(4):
                        fw.matmul(B[7][:, h * 128:(h + 1) * 128], XT[xi][:, h, :], T.Y[:, h, :])
                    V("tensor_tensor", out=X[1 - xi][:], in0=X[xi][:], in1=v4(B[7]), op=ALU.add)
                    if li < 2:
                        for h in range(4):
                            fw.matmul(B[6][:, h * 128:(h + 1) * 128], T.O[:, h, :], XT[xi][:, h, :])
                        V("tensor_copy", out=T.YT[:], in_=v4(B[6]))
                        for h in range(4):
                            fw.matmul(B[1][:, h * 128:(h + 1) * 128], X[xi][:, h, :], T.YT[:, h, :])
                        V("tensor_tensor", out=XT[1 - xi][:], in0=XT[xi][:], in1=v4(B[1]), op=ALU.add)
                    xi = 1 - xi
                R = T.R
                V("tensor_copy", out=R[0][:], in_=X[xi][:])
                Rf = R[0]
                for h in range(4):
                    fw.matmul(B[5][:, h * 128:(h + 1) * 128], T.kg[:, h, :], Sbf[:, h, :])
                V("tensor_tensor", out=T.rr[:], in0=gv_[:], in1=v4(B[5]), op=ALU.subtract)
                for h in range(4):
                    fw.matmul(B[6][:, h * 128:(h + 1) * 128], Rf[:, h, :], T.rr[:, h, :])
                V("tensor_tensor", out=T.vn[:], in0=v4(B[6]), in1=bc_last(beta, 128), op=ALU.mult)
                for h in range(4):
                    fw.matmul(obank[:, h * 128:(h + 1) * 128], T.qg[:, h, :], Sbf[:, h, :], start=True, stop=False)
                    fw.matmul(obank[:, h * 128:(h + 1) * 128], T.QKET[:, h, :], T.vn[:, h, :], start=False, stop=True)
                V("tensor_tensor", out=T.kt[:], in0=gkt_[:], in1=T.ET[:, :, last:last + 1].broadcast_to([128, 4, 128]), op=ALU.mult)
                for h in range(4):
                    fw.matmul(B[7][:, h * 128:(h + 1) * 128], T.kt[:, h, :], T.vn[:, h, :])
                for h in range(4):
                    V("scalar_tensor_tensor", out=S32[:, h, :], in0=S32[:, h, :], scalar=T.eG[:, h, last:last + 1],
                      in1=B[7][:, h * 128:(h + 1) * 128], op0=ALU.mult, op1=ALU.add)
                V("tensor_copy", out=Sbf[:], in_=S32[:])

            def ret_state(r, rkt_, rv_, S32, Sbf):
                V("tensor_tensor", out=kd[:], in0=rkt_[:], in1=bc_last(kdc[:, r * 4:(r + 1) * 4], 128), op=ALU.mult)
                for h in range(4):
                    fw.matmul(B[1][:, h * 128:(h + 1) * 128], kd[:, h, :], rv_[:, h, :])
                for h in range(4):
                    V("scalar_tensor_tensor", out=S32[:, h, :], in0=S32[:, h, :], scalar=cd[:, r * 4 + h:r * 4 + h + 1],
                      in1=B[1][:, h * 128:(h + 1) * 128], op0=ALU.mult, op1=ALU.add)
                V("tensor_copy", out=Sbf[:], in_=S32[:])

            def zero_states():
                for t_ in (Sr, Sg):
                    V("memset", ap=t_[:], constant=0.0)
                for t_ in (Srb, Sgb):
                    V("memset", ap=t_[:], constant=0.0)

            zero_states()
            for it, n in enumerate(reversed(range(NCH))):
                i2 = it % 2
                fw.dma("sync", rq[i2][:], self.RQ[n], semkey=("l_rq", i2))
                fw.dma("sync", rkt[i2][:], self.RKt[n], semkey=("l_rkt", i2))
                fw.dma("sync", rv[i2][:], self.RV[n].rearrange("p (h e) -> p h e", h=4), semkey=("l_rv", i2))
                fw.dma("sync", gq[i2][:], self.GQ[n], semkey=("l_gq", i2))
                fw.dma("sync", gk[i2][:], self.GK[n], semkey=("l_gk", i2))
                fw.dma("sync", gkt[i2][:], self.GKt[n], semkey=("l_gkt", i2))
                fw.dma("sync", gv[i2][:], self.GV[n], semkey=("l_gv", i2))
                fw.dma("sync", gbt[i2][:], self.GB[n], semkey=("l_gb", i2))
                V("tensor_tensor", out=qd[:], in0=rq[i2][:], in1=QDB[:], op=ALU.mult)
                for h in range(4):
                    fw.matmul(B[0][:, h * 128:(h + 1) * 128], qd[:, h, :], Srb[:, h, :])
                A(out=obt[i2][:, 0:512], in_=B[0][:], func=AF.Copy)
                ret_state(1, rkt[i2], rv[i2], Sr, Srb)
                gdn_chunk(1, gq[i2], gk[i2], gkt[i2], gv[i2], gbt[i2], Sg, Sgb, B[0])
                A(out=obt[i2][:, 512:1024], in_=B[0][:], func=AF.Copy)
                fw.dma("sync", self.OB[n], obt[i2][:], semkey=("st_ob", i2))
            zero_states()
            for it, n in enumerate(range(NCH if MSTOP > 3 else 0)):
                i2 = it % 2
                fw.dma("sync", rq[i2][:], self.RQ[n], semkey=("l_rq", i2))
                fw.dma("sync", rk[i2][:], self.RK[n], semkey=("l_rk", i2))
                fw.dma("sync", rkt[i2][:], self.RKt[n], semkey=("l_rkt", i2))
                fw.dma("sync", rv[i2][:], self.RV[n].rearrange("p (h e) -> p h e", h=4), semkey=("l_rv", i2))
                fw.dma("sync", rg[i2][:], self.RGt[n], semkey=("l_rg", i2))
                fw.dma("sync", gq[i2][:], self.GQ[n], semkey=("l_gq", i2))
                fw.dma("sync", gk[i2][:], self.GK[n], semkey=("l_gk", i2))
                fw.dma("sync", gkt[i2][:], self.GKt[n], semkey=("l_gkt", i2))
                fw.dma("sync", gv[i2][:], self.GV[n], semkey=("l_gv", i2))
                fw.dma("sync", gz[i2][:], self.GZ[n], semkey=("l_gz", i2))
                fw.dma("sync", gbt[i2][:], self.GB[n], semkey=("l_gb", i2))
                fw.dma("sync", obt[i2][:], self.OB[n], semkey=("l_ob", i2))
                row0 = b * S + n * 128
                fw.dma("sync", xc[i2][:, 0, :], self.y[row0:row0 + 128, :], semkey=("l_xc", i2))
                for h in range(4):
                    fw.matmul(B[0][:, h * 128:(h + 1) * 128], rk[i2][:, h, :], rq[i2][:, h, :])
                V("tensor_tensor", out=sT[:], in0=B[0][:, :].rearrange("p (h t) -> p h t", h=4), in1=DT[:], op=ALU.mult)
                V("tensor_tensor", out=qd[:], in0=rq[i2][:], in1=QDF[:], op=ALU.mult)
                for h in range(4):
                    fw.matmul(B[0][:, h * 128:(h + 1) * 128], sT[:, h, :], rv[i2][:, h, :], start=True, stop=False)
                    fw.matmul(B[0][:, h * 128:(h + 1) * 128], qd[:, h, :], Srb[:, h, :], start=False, stop=True)
                V("tensor_tensor", out=otot[:, 0:512], in0=B[0][:], in1=obt[i2][:, 0:512], op=ALU.add)
                ret_state(0, rkt[i2], rv[i2], Sr, Srb)
                for h in range(4):
                    V("bn_stats", out=st6[:, h, :], in_=otot[:, h * 128:(h + 1) * 128])
                    V("bn_aggr", out=mv[:, h, :], in_=st6[:, h, :])
                V("tensor_scalar", out=rstd[:, 0:4], in0=mv[:, :, 1], scalar1=1e-6, scalar2=None, op0=ALU.add)
                G("tensor_tensor", out=rstd[:, 0:4], in0=rstd[:, 0:4], in1=self.mhalf[:, 0:4], op=ALU.pow)
                for h in range(4):
                    V("tensor_scalar", out=tmpn[:, h * 128:(h + 1) * 128], in0=otot[:, h * 128:(h + 1) * 128],
                      scalar1=mv[:, h, 0:1], scalar2=rstd[:, h:h + 1], op0=ALU.subtract, op1=ALU.mult)
                V("tensor_tensor", out=og[:, 0:512], in0=tmpn[:], in1=rg[i2][:], op=ALU.mult)
                gdn_chunk(0, gq[i2], gk[i2], gkt[i2], gv[i2], gbt[i2], Sg, Sgb, B[0])
                V("tensor_tensor", out=otot[:, 512:1024], in0=B[0][:], in1=obt[i2][:, 512:1024], op=ALU.add)
                for h in range(4):
                    A(out=self.junk[:, h * 128:(h + 1) * 128], in_=otot[:, 512 + h * 128:512 + (h + 1) * 128], func=AF.Square,
                      accum_out=ssg[:, h:h + 1])
                V("tensor_scalar", out=rstd[:, 4:8], in0=ssg[:], scalar1=1.0 / 128.0, scalar2=1e-6, op0=ALU.mult, op1=ALU.add)
                G("tensor_tensor", out=rstd[:, 4:8], in0=rstd[:, 4:8], in1=self.mhalf[:, 0:4], op=ALU.pow)
                for h in range(4):
                    V("scalar_tensor_tensor", out=tmpn[:, h * 128:(h + 1) * 128], in0=otot[:, 512 + h * 128:512 + (h + 1) * 128],
                      scalar=rstd[:, 4 + h:5 + h], in1=gnw[:], op0=ALU.mult, op1=ALU.mult)
                V("tensor_tensor", out=og[:, 512:1024], in0=tmpn[:], in1=gz[i2][:], op=ALU.mult)
                tb = B[1].bitcast(BF16)
                for j in range(8):
                    fw.transpose(tb[:, j * 128:(j + 1) * 128], og[:, j * 128:(j + 1) * 128], self.identb[:])
                A(out=ogT[:], in_=tb[:, :].rearrange("p (j t) -> p j t", j=8), func=AF.Copy)
                for hf, pb in ((0, B[2]), (1, B[3])):
                    for k in range(8):
                        fw.matmul(pb[:], ogT[:, k, :], wout[:, k, hf * 512:(hf + 1) * 512], start=(k == 0), stop=(k == 7))
                self.post_resid(xc[i2], 0, B[2], B[3], 1.0)
                fw.dma("sync", self.y[row0:row0 + 128, :], xc[i2][:, 0, :], semkey=("st_y", i2))
    fw.fence()


K.mix_phase = mix_phase
K.mix_consts = mix_consts
K.rot_tables = rot_tables


def make_in_maps(inp, cfg, ncores):
    L = cfg.DEPTH
    maps = []
    for c in range(ncores):
        b0 = c * cfg.NS
        m = {
            "x": np.ascontiguousarray(inp["x"][b0:b0 + cfg.NS]).reshape(cfg.NT, D),
            "mem": np.ascontiguousarray(inp["mem"][b0:b0 + cfg.NS]).reshape(cfg.NS * MEM, D),
            "positions": np.ascontiguousarray(inp["positions"][b0:b0 + cfg.NS]).reshape(cfg.NT),
            "norm_pre": inp["norm_pre"][:L].reshape(L * 4, D),
            "norm_post": inp["norm_post"][:L].reshape(L * 4, D),
            "mem_norm": inp["mem_norm"][:L].reshape(L, D),
            "gdn_conv": inp["gdn_conv"][:L].reshape(L * 5, 1536),
            "ret_log_gamma": inp["ret_log_gamma"][:L].reshape(L, 8),
            "gdn_a_log": inp["gdn_a_log"][:L].reshape(L, 8),
            "gdn_dt_bias": inp["gdn_dt_bias"][:L].reshape(L, 8),
            "gdn_norm": inp["gdn_norm"][:L].reshape(L, 128),
        }
        for n, r, cc in WNAMES:
            m[n] = inp[n][:L].reshape(L * r, cc)
        maps.append({k: np.ascontiguousarray(v) for k, v in m.items()})
    return maps


def kernel(**inputs):
    cfg = Cfg(NS=2, S=4096, DEPTH=4)
    inp = {k: np.asarray(v) for k, v in inputs.items()}
    k = K(cfg)
    k.build()
    in_maps = make_in_maps(inp, cfg, 8)
    res = run_bass_kernel_spmd(k.nc, in_maps, core_ids=list(range(8)))
    out = np.concatenate([np.asarray(r["y"]).reshape(cfg.NS, cfg.S, D) for r in res.results], axis=0)
    return out.astype(np.float32)
```

```python
import math
import os
import numpy as np
import os
import concourse.bass as bass
import concourse.mybir as mybir

F32 = mybir.dt.float32
BF16 = mybir.dt.bfloat16
I32 = mybir.dt.int32
ALU = mybir.AluOpType
AF = mybir.ActivationFunctionType
AX = mybir.AxisListType

COMPUTE = ("tensor", "vector", "scalar", "gpsimd")
SAME_ENGINE_SYNC = bool(int(os.environ.get("SES", "1")))


def _region(ap):
    t = ap.tensor
    es = mybir.dt.size(ap.dtype)
    pat = [(st * es, n) for st, n in ap.ap]
    off = ap.offset * es
    if type(t).__name__ == "DRamTensorHandle":
        lo = off
        hi = off
        for st, n in pat:
            if st >= 0:
                hi += st * (n - 1)
            else:
                lo += st * (n - 1)
        return (t.name, 0, 1, lo, hi + es)
    if type(t).__name__ == "PSumTensorHandle":
        return (t.name, 0, 128, 0, 2048)
    pst, pn = pat[0]
    if pst == 0:
        pst = 1 << 40
    p0 = off // pst if pst < (1 << 40) else 0
    f = off - p0 * pst if pst < (1 << 40) else off
    lo = f
    hi = f
    for st, n in pat[1:]:
        if st >= 0:
            hi += st * (n - 1)
        else:
            lo += st * (n - 1)
    return (t.name, p0, p0 + pn, lo, hi + es)


def _overlap(a, b):
    return a[1] < b[2] and b[1] < a[2] and a[3] < b[4] and b[3] < a[4]


def _covers(a, b):
    return a[1] <= b[1] and a[2] >= b[2] and a[3] <= b[3] and a[4] >= b[4]


class Op:
    __slots__ = ("eng", "fn", "deps", "idx", "marked", "cnt", "is_dma", "semkey", "dval", "kind")

    def __init__(self, eng, fn, is_dma=False, semkey=None, kind=""):
        self.eng = eng
        self.fn = fn
        self.deps = {}
        self.marked = False
        self.cnt = 0
        self.is_dma = is_dma
        self.semkey = semkey
        self.dval = 0
        self.kind = kind


class FW:
    def __init__(self, nc):
        self.nc = nc
        self.ops = []
        self.recs = {}
        self.dma_cnt = {}
        self.dma_last = {}
        self.n_auto = 0

    def sb(self, name, shape, dt=F32):
        return self.nc.alloc_sbuf_tensor(name, list(shape), dt)

    def ps(self, name, shape, dt=F32):
        return self.nc.alloc_psum_tensor(name, list(shape), dt)

    def dram(self, name, shape, dt=F32, kind="Internal"):
        return self.nc.dram_tensor(name, list(shape), dt, kind=kind)

    def _access(self, op, ap, is_write):
        reg = _region(ap)
        recs = self.recs.setdefault(reg[0], [])
        is_psum = type(ap.tensor).__name__ == "PSumTensorHandle"
        new = []
        for (r, o, w) in recs:
            keep = True
            if _overlap(r, reg) and (w or is_write or (is_psum and o.eng != op.eng)):
                if o is not op:
                    self._dep(op, o)
                if is_write and _covers(reg, r):
                    keep = False
            if keep and (not is_write) and (not w) and o.eng == op.eng and (not o.is_dma) and (not op.is_dma) and r == reg:
                keep = False
            if keep:
                new.append((r, o, w))
        new.append((reg, op, is_write))
        self.recs[reg[0]] = new

    def _dep(self, op, o):
        if o.is_dma:
            key = ("dma", id(o))
            op.deps[key] = o
        else:
            if o.eng == op.eng and not op.is_dma and (o.eng == "tensor" or not SAME_ENGINE_SYNC):
                return
            prev = op.deps.get(o.eng)
            if prev is None or prev.idx < o.idx:
                op.deps[o.eng] = o

    def add(self, eng, fn, reads=(), writes=(), is_dma=False, semkey=None, kind=""):
        op = Op(eng, fn, is_dma, semkey, kind)
        op.idx = len(self.ops)
        self._apply_fence(op)
        for ap in reads:
            self._access(op, ap, False)
        for ap in writes:
            self._access(op, ap, True)
        if is_dma:
            last = self.dma_last.get(semkey)
            if last is not None:
                op.deps[("dma", id(last))] = last
            self.dma_cnt[semkey] = self.dma_cnt.get(semkey, 0) + 1
            op.dval = 16 * self.dma_cnt[semkey]
            self.dma_last[semkey] = op
        self.ops.append(op)
        return op

    def fence(self):
        last = {}
        for op in self.ops:
            if not op.is_dma:
                last[op.eng] = op
        dl = [d for k_, d in self.dma_last.items() if not (isinstance(k_, tuple) and k_[0] == "cast")]
        self._fence = (last, dl)
        self._fence_pending = {e: True for e in ("tensor", "vector", "scalar", "gpsimd", "sync")}

    def _apply_fence(self, op):
        f = getattr(self, "_fence", None)
        if f is None or not self._fence_pending.get(op.eng):
            return
        self._fence_pending[op.eng] = False
        last, dl = f
        for e, o in last.items():
            if e == op.eng and not op.is_dma:
                continue
            prev = op.deps.get(e)
            if prev is None or prev.idx < o.idx:
                op.deps[e] = o
        for d in dl:
            op.deps[("dma", id(d))] = d

    def I(self, eng, meth, *, out=None, outs=(), ins=(), **kw):
        reads = list(ins)
        writes = list(outs)
        kwargs = dict(kw)
        if out is not None:
            kwargs["out"] = out
            writes.append(out)
        for k, v in kw.items():
            if type(v).__name__ == "AP":
                if k in ("accum_out", "ap"):
                    writes.append(v)
                else:
                    reads.append(v)
        e = getattr(self.nc, eng)

        def fn():
            return getattr(e, meth)(**kwargs)
        return self.add(eng, fn, reads, writes, kind=meth)

    def matmul(self, out, lhsT, rhs, start=True, stop=True, **kw):
        e = self.nc.tensor

        def fn():
            return e.matmul(out, lhsT, rhs, start=start, stop=stop, **kw)
        return self.add("tensor", fn, [lhsT, rhs], [out], kind="matmul")

    def transpose(self, out, in_, ident):
        e = self.nc.tensor

        def fn():
            return e.transpose(out, in_, ident)
        return self.add("tensor", fn, [in_, ident], [out], kind="transpose")

    def dma(self, q, out, in_, semkey=None, **kw):
        e = getattr(self.nc, q)
        if semkey is None:
            semkey = out.tensor.name.split("__u")[0]

        def fn():
            return e.dma_start(out=out, in_=in_, **kw)
        return self.add(q, fn, [in_], [out], is_dma=True, semkey=semkey, kind="dma")

    def finalize(self):
        from contextlib import ExitStack
        nc = self.nc
        ENGS = ("tensor", "vector", "scalar", "gpsimd", "sync")
        for op in self.ops:
            for d in op.deps.values():
                d.marked = True
        cnt = {e: 0 for e in ENGS}
        for op in self.ops:
            if not op.is_dma and op.marked:
                cnt[op.eng] += 1
                op.cnt = cnt[op.eng]
        semkeys = {}
        waited = {e: {} for e in ENGS}
        nwait = 0
        per_eng = {e: [] for e in ENGS}
        for op in self.ops:
            w = waited[op.eng]
            waits = []
            for key, d in op.deps.items():
                if d.is_dma:
                    sk = ("dma", d.semkey)
                    val = d.dval
                else:
                    sk = ("eng", d.eng)
                    val = d.cnt
                if w.get(sk, 0) >= val:
                    continue
                w[sk] = val
                semkeys.setdefault(sk, len(semkeys))
                waits.append((sk, val))
                nwait += 1
            if op.is_dma:
                semkeys.setdefault(("dma", op.semkey), len(semkeys))
            elif op.marked:
                semkeys.setdefault(("eng", op.eng), len(semkeys))
            per_eng[op.eng].append((op, waits))
        final = []
        for key, last in self.dma_last.items():
            sk = ("dma", key)
            if waited["sync"].get(sk, 0) < last.dval:
                final.append((sk, last.dval))
        self.stats = dict(n_ops=len(self.ops), n_wait=nwait, n_sems=len(semkeys),
                          per_eng={e: len(v) for e, v in per_eng.items()})
        with ExitStack() as st:
            sems = {}
            for sk, i in semkeys.items():
                sems[sk] = st.enter_context(nc.semaphore("s%d" % i))
            block = st.enter_context(nc.Block())

            def body(ename):
                def f(eng):
                    for op, waits in per_eng[ename]:
                        for sk, val in waits:
                            eng.wait_ge(sems[sk], val)
                        inst = op.fn()
                        if op.is_dma:
                            inst.then_inc(sems[("dma", op.semkey)], 16)
                        elif op.marked:
                            inst.then_inc(sems[("eng", op.eng)], 1)
                    if ename == "sync":
                        for sk, val in final:
                            eng.wait_ge(sems[sk], val)
                return f
            for ename in ENGS:
                if per_eng[ename] or ename == "sync":
                    getattr(block, ename)(body(ename))
        return self.stats


import numpy as np
from contextlib import ExitStack
import concourse.bass as bass
import concourse.mybir as mybir
from concourse.bass_utils import run_bass_kernel_spmd

D = 1024
DFF = 2816
NFF = DFF // 128
INC = 4112
MEM = 256
EPS = 1e-6
TT = 512


class Cfg:
    def __init__(self, NS=2, S=4096, DEPTH=4, phases=("ffn1", "mix", "xattn", "ffn2")):
        self.NS, self.S, self.DEPTH, self.phases = NS, S, DEPTH, phases
        self.NT = NS * S


WNAMES = [("ffn1_gate", D, DFF), ("ffn1_up", D, DFF), ("ffn1_down", DFF, D), ("w_in", D, INC),
          ("w_out", D, D), ("xattn_q", D, D), ("xattn_k", D, D), ("xattn_v", D, D), ("xattn_o", D, D),
          ("ffn2_gate", D, DFF), ("ffn2_up", D, DFF), ("ffn2_down", DFF, D)]


class K:
    def __init__(self, cfg):
        self.cfg = cfg
        nc = self.nc = bass.Bass("TRN2", target_bir_lowering=False)
        fw = self.fw = FW(nc)
        L = cfg.DEPTH
        NT = cfg.NT
        ein = lambda n, sh, dt=F32: nc.dram_tensor(n, list(sh), dt, kind="ExternalInput").ap()
        self.x_in = ein("x", [NT, D])
        self.mem = ein("mem", [cfg.NS * MEM, D])
        self.pos = ein("positions", [cfg.NS * cfg.S], I32)
        self.norm_pre = ein("norm_pre", [L * 4, D])
        self.norm_post = ein("norm_post", [L * 4, D])
        self.mem_norm = ein("mem_norm", [L, D])
        self.w32 = {}
        self.wbf = {}
        for n, r, c in WNAMES:
            self.w32[n] = ein(n, [L * r, c])
            self.wbf[n] = nc.dram_tensor(n + "_bf", [L * r, c], BF16, kind="Internal").ap()
        self.gdn_conv = ein("gdn_conv", [L * 5, 1536])
        self.ret_lg = ein("ret_log_gamma", [L, 8])
        self.a_log = ein("gdn_a_log", [L, 8])
        self.dt_bias = ein("gdn_dt_bias", [L, 8])
        self.gdn_norm = ein("gdn_norm", [L, 128])
        self.y = nc.dram_tensor("y", [NT, D], F32, kind="ExternalOutput").ap()
        self.bank = [fw.ps("bank%d" % i, [128, 512]) for i in range(8)]
        self.st = ExitStack()
        self.identf = fw.sb("identf", [128, 128])
        self.identb = fw.sb("identb", [128, 128], BF16)
        self.onesf = fw.sb("onesf", [128, 128])
        self.zerosf = fw.sb("zerosf", [128, 128])
        self.junk = fw.sb("junk", [128, 1024])
        self.epsc = fw.sb("epsc", [128, 1])
        self.eps4 = fw.sb("eps4", [128, 1])
        self.mhalf = fw.sb("mhalf", [128, 512])
        self.consts()
        self.mix_consts()

    def uid(self):
        self._uid = getattr(self, "_uid", 0) + 1
        return self._uid

    def V(self, meth, **kw):
        return self.fw.I("vector", meth, **kw)

    def A(self, **kw):
        return self.fw.I("scalar", "activation", **kw)

    def G(self, meth, **kw):
        return self.fw.I("gpsimd", meth, **kw)

    def consts(self):
        fw = self.fw
        self.G("memset", ap=self.onesf[:], constant=1.0)
        self.G("memset", ap=self.zerosf[:], constant=0.0)
        self.G("memset", ap=self.epsc[:], constant=EPS)
        self.G("memset", ap=self.eps4[:], constant=EPS * 4.0)
        self.G("memset", ap=self.mhalf[:], constant=-0.5)
        self.G("affine_select", out=self.identf[:], in_=self.onesf[:], pattern=[[1, 128]],
               compare_op=ALU.is_equal, fill=0.0, base=0, channel_multiplier=-1)
        self.V("tensor_copy", out=self.identb[:], in_=self.identf[:])

    def cast_weights(self, l):
        for n, r, c in WNAMES:
            src = self.w32[n][l * r:(l + 1) * r, :]
            dst = self.wbf[n][l * r:(l + 1) * r, :]
            for r0 in range(0, r, 1024):
                r1 = min(r, r0 + 1024)
                for c0 in range(0, c, 2048):
                    c1 = min(c, c0 + 2048)
                    self._cast_i = getattr(self, "_cast_i", 0) + 1
                    self.fw.dma("gpsimd", dst[r0:r1, c0:c1], src[r0:r1, c0:c1], semkey=("cast", self._cast_i % 13))

    def alloc_common(self, st, nxt=2, post=True):
        nc = self.nc
        sb = lambda n, sh, dt=F32: st.enter_context(nc.sbuf_tensor(n + "__u%d" % self.uid(), list(sh), dt))
        self.xt = [sb("xt%d" % i, [128, 4, D]) for i in range(nxt)]
        self.nxt = nxt
        self.wpre = sb("wpre", [128, D])
        self.wpost = sb("wpost", [128, D])
        self.ub = [sb("ub%d" % i, [128, D], BF16) for i in range(2)]
        self.uT = sb("uT", [128, 8, TT], BF16)
        self.ss = sb("ss", [128, 8])
        self.rs = sb("rs", [128, 8])
        self.ss2 = sb("ss2", [128, 4])
        self.rs2 = sb("rs2", [128, 4])
        if post:
            self.tpost = [sb("tpost%d" % i, [128, D]) for i in range(2)]
        self.tile_i = 0

    def load_norm_w(self, l, sub):
        self.fw.dma("sync", self.wpre[:], self.norm_pre[l * 4 + sub].partition_broadcast(128))
        self.fw.dma("sync", self.wpost[:], self.norm_post[l * 4 + sub].partition_broadcast(128))

    def load_tile(self, row0, src=None):
        src = self.y if src is None else src
        xt = self.xt[self.tile_i % self.nxt]
        self.tile_i += 1
        self.fw.dma("sync", xt[:], src[row0:row0 + TT, :].rearrange("(s p) d -> p s d", p=128))
        return xt

    def rstd_of(self, ss, rs, n, scale):
        self.V("tensor_scalar", out=rs[:, 0:n], in0=ss[:, 0:n], scalar1=scale, scalar2=EPS, op0=ALU.mult, op1=ALU.add)
        self.G("tensor_tensor", out=rs[:, 0:n], in0=rs[:, 0:n], in1=self.mhalf[:, 0:n], op=ALU.pow)

    def norm_uT(self, xt, tbank):
        for s in range(4):
            self.A(out=self.junk[:], in_=xt[:, s, :], func=AF.Square, accum_out=self.ss[:, s:s + 1])
        self.rstd_of(self.ss, self.rs, 4, 1.0 / D)
        for s in range(4):
            ub = self.ub[s % 2]
            self.V("scalar_tensor_tensor", out=ub[:], in0=xt[:, s, :], scalar=self.rs[:, s:s + 1],
                   in1=self.wpre[:], op0=ALU.mult, op1=ALU.mult)
            tb = tbank[s % 2].bitcast(BF16)
            for j in range(8):
                self.fw.transpose(tb[:, j * 128:(j + 1) * 128], ub[:, j * 128:(j + 1) * 128], self.identb[:])
            self.A(out=self.uT[:, :, s * 128:(s + 1) * 128],
                   in_=tb[:, :].rearrange("p (j t) -> p j t", j=8), func=AF.Copy)

    def post_resid(self, xt, s, pb0, pb1, factor):
        self.A(out=self.junk[:, 0:512], in_=pb0[:], func=AF.Square, accum_out=self.ss2[:, 0:1])
        self.A(out=self.junk[:, 512:1024], in_=pb1[:], func=AF.Square, accum_out=self.ss2[:, 1:2])
        self.V("tensor_tensor", out=self.ss2[:, 2:3], in0=self.ss2[:, 0:1], in1=self.ss2[:, 1:2], op=ALU.add)
        f2 = 1.0 / (factor * factor)
        self.V("tensor_scalar", out=self.rs2[:, 0:1], in0=self.ss2[:, 2:3], scalar1=f2 / D, scalar2=EPS * f2, op0=ALU.mult, op1=ALU.add)
        self.G("tensor_tensor", out=self.rs2[:, 1:2], in0=self.rs2[:, 0:1], in1=self.mhalf[:, 0:1], op=ALU.pow)
        tp = self.tpost[s % 2]
        self.V("scalar_tensor_tensor", out=tp[:, 0:512], in0=pb0[:], scalar=self.rs2[:, 1:2],
               in1=self.wpost[:, 0:512], op0=ALU.mult, op1=ALU.mult)
        self.V("scalar_tensor_tensor", out=tp[:, 512:1024], in0=pb1[:], scalar=self.rs2[:, 1:2],
               in1=self.wpost[:, 512:1024], op0=ALU.mult, op1=ALU.mult)
        self.V("tensor_tensor", out=xt[:, s, :], in0=xt[:, s, :], in1=tp[:], op=ALU.add)

    def epsf(self, f2):
        key = "epsf%g" % f2
        if not hasattr(self, "_epsf"):
            self._epsf = {}
        if key not in self._epsf:
            t = self.fw.sb("epsf%d" % len(self._epsf), [128, 1])
            self.G("memset", ap=t[:], constant=EPS * f2)
            self._epsf[key] = t
        return self._epsf[key][:, 0:1]

    def store_tile(self, xt, row0):
        self.fw.dma("sync", self.y[row0:row0 + TT, :].rearrange("(s p) d -> p s d", p=128), xt[:],
                    semkey=("ystore", self.tile_i % 2))

    def ffn_phase(self, l, which):
        cfg, fw, nc = self.cfg, self.fw, self.nc
        pre = "ffn%d_" % which
        sub = 0 if which == 1 else 3
        fw.fence()
        with ExitStack() as st:
            sb = lambda n, sh, dt=F32: st.enter_context(nc.sbuf_tensor(n + "__u%d" % self.uid(), list(sh), dt))
            self.alloc_common(st)
            wd = sb("wd", [128, NFF, D], BF16)
            slab = [sb("slab%d" % i, [128, 2, 8, 256], BF16) for i in range(3)]
            h1T = sb("h1T", [128, NFF, TT], BF16)
            sg = [sb("sg%d" % i, [128, TT]) for i in range(2)]
            self.load_norm_w(l, sub)
            wdsrc = self.wbf[pre + "down"][l * DFF:(l + 1) * DFF, :].rearrange("(c p) n -> p c n", p=128)
            fw.dma("sync", wd[:, 0:11, :], wdsrc[:, 0:11, :])
            fw.dma("sync", wd[:, 11:22, :], wdsrc[:, 11:22, :], semkey="wd_b")
            wg = self.wbf[pre + "gate"][l * D:(l + 1) * D, :].rearrange("(k p) n -> p k n", p=128)
            wu = self.wbf[pre + "up"][l * D:(l + 1) * D, :].rearrange("(k p) n -> p k n", p=128)
            B = self.bank
            sl_i = 0
            nxt = self.load_tile(0)
            for t in range(cfg.NT // TT):
                xt = nxt
                self.norm_uT(xt, [B[6], B[7]])
                if t + 1 < cfg.NT // TT:
                    nxt = self.load_tile((t + 1) * TT)
                for sl in range(NFF // 2):
                    sbuf = slab[sl_i % 3]
                    sl_i += 1
                    fw.dma("sync", sbuf[:, 0, :, :], wg[:, :, sl * 256:(sl + 1) * 256], semkey=("slab", sl_i % 3, 0))
                    fw.dma("sync", sbuf[:, 1, :, :], wu[:, :, sl * 256:(sl + 1) * 256], semkey=("slab", sl_i % 3, 1))
                    for cc in range(2):
                        c = sl * 2 + cc
                        pg, pu = B[(c % 2) * 2], B[(c % 2) * 2 + 1]
                        for k in range(8):
                            fw.matmul(pg[:], sbuf[:, 0, k, cc * 128:(cc + 1) * 128], self.uT[:, k, :], start=(k == 0), stop=(k == 7))
                        for k in range(8):
                            fw.matmul(pu[:], sbuf[:, 1, k, cc * 128:(cc + 1) * 128], self.uT[:, k, :], start=(k == 0), stop=(k == 7))
                        self.A(out=sg[c % 2][:], in_=pg[:], func=AF.Tanh, scale=0.5)
                        self.V("scalar_tensor_tensor", out=sg[c % 2][:], in0=sg[c % 2][:], scalar=1.0, in1=pg[:], op0=ALU.add, op1=ALU.mult)
                        self.V("scalar_tensor_tensor", out=h1T[:, c, :], in0=sg[c % 2][:], scalar=0.5, in1=pu[:], op0=ALU.mult, op1=ALU.mult)
                for s in range(4):
                    pb0, pb1 = B[4 + (s % 2) * 2], B[5 + (s % 2) * 2]
                    for hf, pb in ((0, pb0), (1, pb1)):
                        for c in range(NFF):
                            fw.matmul(pb[:], h1T[:, c, s * 128:(s + 1) * 128], wd[:, c, hf * 512:(hf + 1) * 512],
                                      start=(c == 0), stop=(c == NFF - 1))
                    self.post_resid(xt, s, pb0, pb1, 0.5)
                self.store_tile(xt, t * TT)
        fw.fence()

    def xattn_phase(self, l):
        cfg, fw, nc = self.cfg, self.fw, self.nc
        fw.fence()
        with ExitStack() as st:
            sb = lambda n, sh, dt=F32: st.enter_context(nc.sbuf_tensor(n + "__u%d" % self.uid(), list(sh), dt))
            self.alloc_common(st)
            wq = sb("wq", [128, 8, D], BF16)
            wo = sb("wo", [128, 8, D], BF16)
            wk = sb("wk", [128, 8, D], BF16)
            wv = sb("wv", [128, 8, D], BF16)
            mnw = sb("mnw", [128, D])
            mt = sb("mt", [128, 2, D])
            mT = sb("mT", [128, 8, MEM], BF16)
            kT = [sb("kT%d" % i, [128, 8, MEM], BF16) for i in range(cfg.NS)]
            vv = [sb("vv%d" % i, [128, 2, D], BF16) for i in range(cfg.NS)]
            qT = sb("qT", [128, 8, TT], BF16)
            mx = sb("mx", [128, 4])
            nmx = sb("nmx", [128, 4])
            sm = sb("sm", [128, 4])
            rsm = sb("rsm", [128, 4])
            pp = sb("pp", [128, 4, MEM], BF16)
            pT = sb("pT", [128, 8, 128], BF16)
            osb = sb("osb", [128, D], BF16)
            oT = sb("oT", [128, 8, 128], BF16)
            self.load_norm_w(l, 2)
            B = self.bank
            for w, n in ((wq, "xattn_q"), (wo, "xattn_o"), (wk, "xattn_k"), (wv, "xattn_v")):
                fw.dma("sync", w[:], self.wbf[n][l * D:(l + 1) * D, :].rearrange("(k p) n -> p k n", p=128))
            fw.dma("sync", mnw[:], self.mem_norm[l].partition_broadcast(128))
            for b in range(cfg.NS):
                fw.dma("sync", mt[:], self.mem[b * MEM:(b + 1) * MEM, :].rearrange("(s p) d -> p s d", p=128))
                for s in range(2):
                    self.A(out=self.junk[:], in_=mt[:, s, :], func=AF.Square, accum_out=self.ss[:, s:s + 1])
                self.rstd_of(self.ss, self.rs, 2, 1.0 / D)
                for s in range(2):
                    ub = self.ub[s % 2]
                    self.V("scalar_tensor_tensor", out=ub[:], in0=mt[:, s, :], scalar=self.rs[:, s:s + 1],
                           in1=mnw[:], op0=ALU.mult, op1=ALU.mult)
                    tb = B[6 + s % 2].bitcast(BF16)
                    for j in range(8):
                        fw.transpose(tb[:, j * 128:(j + 1) * 128], ub[:, j * 128:(j + 1) * 128], self.identb[:])
                    self.A(out=mT[:, :, s * 128:(s + 1) * 128], in_=tb[:, :].rearrange("p (j t) -> p j t", j=8), func=AF.Copy)
                for fc in range(8):
                    pk = B[fc % 2]
                    for k in range(8):
                        fw.matmul(pk[:, 0:MEM], wk[:, k, fc * 128:(fc + 1) * 128], mT[:, k, :], start=(k == 0), stop=(k == 7))
                    self.V("tensor_copy", out=kT[b][:, fc, :], in_=pk[:, 0:MEM])
                for j in range(2):
                    for hf in range(2):
                        pv = B[2 + hf]
                        for k in range(8):
                            fw.matmul(pv[:], mT[:, k, j * 128:(j + 1) * 128], wv[:, k, hf * 512:(hf + 1) * 512], start=(k == 0), stop=(k == 7))
                        self.A(out=vv[b][:, j, hf * 512:(hf + 1) * 512], in_=pv[:], func=AF.Copy)
            import os
            XSTOP = int(os.environ.get("XSTOP", "9"))
            nxt = self.load_tile(0)
            for t in range(cfg.NT // TT if XSTOP > 0 else 0):
                b = (t * TT) // cfg.S
                xt = nxt
                self.norm_uT(xt, [B[6], B[7]])
                if t + 1 < cfg.NT // TT:
                    nxt = self.load_tile((t + 1) * TT)
                for fc in range(8):
                    pq = B[fc % 2]
                    for k in range(8):
                        fw.matmul(pq[:], wq[:, k, fc * 128:(fc + 1) * 128], self.uT[:, k, :], start=(k == 0), stop=(k == 7))
                    self.A(out=qT[:, fc, :], in_=pq[:], func=AF.Identity, scale=1.0 / 16.0)
                for s in range(4 if XSTOP > 1 else 0):
                    for h in range(4):
                        ps = B[2 + h // 2][:, (h % 2) * MEM:(h % 2 + 1) * MEM]
                        for j in range(2):
                            fw.matmul(ps, qT[:, 2 * h + j, s * 128:(s + 1) * 128], kT[b][:, 2 * h + j, :], start=(j == 0), stop=(j == 1))
                    for h2 in range(2):
                        self.V("tensor_reduce", out=mx[:, 2 * h2:2 * h2 + 2], in_=B[2 + h2][:, :].rearrange("p (h m) -> p h m", h=2),
                               axis=AX.X, op=ALU.max)
                    self.V("tensor_scalar", out=nmx[:], in0=mx[:], scalar1=-1.0, scalar2=None, op0=ALU.mult)
                    for h in range(4):
                        ps = B[2 + h // 2][:, (h % 2) * MEM:(h % 2 + 1) * MEM]
                        self.A(out=pp[:, h, :], in_=ps, func=AF.Exp, bias=nmx[:, h:h + 1], scale=1.0, accum_out=sm[:, h:h + 1])
                    self.V("reciprocal", out=rsm[:], in_=sm[:])
                    if XSTOP <= 2:
                        continue
                    tb = B[6 + s % 2].bitcast(BF16)
                    for h in range(4):
                        for j in range(2):
                            fw.transpose(tb[:, (2 * h + j) * 128:(2 * h + j + 1) * 128], pp[:, h, j * 128:(j + 1) * 128], self.identb[:])
                    self.A(out=pT[:], in_=tb[:, :].rearrange("p (j t) -> p j t", j=8), func=AF.Copy)
                    for h in range(4):
                        po = B[h // 2][:, (h % 2) * 256:(h % 2 + 1) * 256]
                        for j in range(2):
                            fw.matmul(po, pT[:, 2 * h + j, :], vv[b][:, j, h * 256:(h + 1) * 256], start=(j == 0), stop=(j == 1))
                    for h in range(4):
                        po = B[h // 2][:, (h % 2) * 256:(h % 2 + 1) * 256]
                        self.V("tensor_scalar", out=osb[:, h * 256:(h + 1) * 256], in0=po, scalar1=rsm[:, h:h + 1], scalar2=None, op0=ALU.mult)
                    if XSTOP <= 3:
                        continue
                    tb2 = B[4 + s % 2].bitcast(BF16)
                    for j in range(8):
                        fw.transpose(tb2[:, j * 128:(j + 1) * 128], osb[:, j * 128:(j + 1) * 128], self.identb[:])
                    self.A(out=oT[:], in_=tb2[:, :].rearrange("p (j t) -> p j t", j=8), func=AF.Copy)
                    pb0, pb1 = B[0], B[1]
                    for hf, pb in ((0, pb0), (1, pb1)):
                        for k in range(8):
                            fw.matmul(pb[:], oT[:, k, :], wo[:, k, hf * 512:(hf + 1) * 512], start=(k == 0), stop=(k == 7))
                    self.post_resid(xt, s, pb0, pb1, 1.0)
                self.store_tile(xt, t * TT)
        fw.fence()

    def build(self):
        cfg, fw = self.cfg, self.fw
        fw.dma("sync", self.y[:, :], self.x_in[:, :], semkey="xcopy")
        if "mix" in cfg.phases:
            self.rot_tables()
        self.cast_weights(0)
        for l in range(cfg.DEPTH):
            for pi, ph in enumerate(cfg.phases):
                if pi == 1 or (pi == 0 and len(cfg.phases) == 1):
                    if l + 1 < cfg.DEPTH and not getattr(self, "_cast_done_%d" % (l + 1), False):
                        setattr(self, "_cast_done_%d" % (l + 1), True)
                        self.cast_weights(l + 1)
                if ph == "ffn1":
                    self.ffn_phase(l, 1)
                elif ph == "ffn2":
                    self.ffn_phase(l, 2)
                elif ph == "xattn":
                    self.xattn_phase(l)
                elif ph == "mix":
                    self.mix_phase(l)
        with self.nc.allow_non_contiguous_dma(reason="small strided parameter loads"):
            stats = fw.finalize()
        return stats


PI = math.pi


def bc_mid(ap2d, n):
    return ap2d.unsqueeze(1).broadcast_to([ap2d.shape[0], n, ap2d.shape[1]])


def bc_last(ap2d, n):
    return ap2d.unsqueeze(2).broadcast_to([ap2d.shape[0], ap2d.shape[1], n])


def mix_consts(self):
    fw = self.fw
    sbt = fw.sb
    self.NM_lo = sbt("NM_lo", [128, 128])
    self.NM_up = sbt("NM_up", [128, 128])
    self.OFFD = sbt("OFFD", [128, 128])
    self.UP1 = sbt("UP1", [128, 128])
    self.LO1 = sbt("LO1", [128, 128])
    self.PTb = sbt("PTb", [128, 128], BF16)
    self.onesb = sbt("onesb", [128, 128], BF16)
    self.POS = sbt("POS", [128, 128])
    self.NEG = sbt("NEG", [128, 128])
    self.ROW1 = sbt("ROW1", [128, 128])
    self.ROWB = sbt("ROWB", [128, 128])
    self.COLF = sbt("COLF", [128, 1])
    self.COLB = sbt("COLB", [128, 1])
    self.invf = sbt("invf", [128, 1])
    self.qsc = sbt("qsc", [128, 1])
    itmp = sbt("itmp", [128, 128], I32)
    ftmp = sbt("ftmp", [128, 128])
    G, V, A = self.G, self.V, self.A
    G("affine_select", out=self.NM_lo[:], in_=self.zerosf[:], pattern=[[-1, 128]], compare_op=ALU.is_ge, fill=-1e30, base=0, channel_multiplier=1)
    G("affine_select", out=self.NM_up[:], in_=self.zerosf[:], pattern=[[1, 128]], compare_op=ALU.is_ge, fill=-1e30, base=0, channel_multiplier=-1)
    G("affine_select", out=self.LO1[:], in_=self.onesf[:], pattern=[[-1, 128]], compare_op=ALU.is_ge, fill=0.0, base=0, channel_multiplier=1)
    G("affine_select", out=self.UP1[:], in_=self.onesf[:], pattern=[[1, 128]], compare_op=ALU.is_ge, fill=0.0, base=0, channel_multiplier=-1)
    G("affine_select", out=self.OFFD[:], in_=self.onesf[:], pattern=[[1, 128]], compare_op=ALU.not_equal, fill=0.0, base=0, channel_multiplier=-1)
    G("affine_select", out=ftmp[:], in_=self.onesf[:], pattern=[[1, 128]], compare_op=ALU.is_equal, fill=0.0, base=-64, channel_multiplier=-1)
    G("affine_select", out=self.POS[:], in_=self.onesf[:], pattern=[[1, 128]], compare_op=ALU.is_equal, fill=0.0, base=64, channel_multiplier=-1)
    V("tensor_tensor", out=self.PTb[:], in0=ftmp[:], in1=self.POS[:], op=ALU.subtract)
    V("tensor_copy", out=self.onesb[:], in_=self.onesf[:])
    G("iota", out=itmp[:], pattern=[[1, 128]], base=0, channel_multiplier=-1)
    V("tensor_copy", out=ftmp[:], in_=itmp[:])
    V("tensor_scalar", out=self.POS[:], in0=ftmp[:], scalar1=0.0, scalar2=None, op0=ALU.max)
    V("tensor_scalar", out=self.NEG[:], in0=ftmp[:], scalar1=-1.0, scalar2=0.0, op0=ALU.mult, op1=ALU.max)
    G("iota", out=itmp[:], pattern=[[1, 128]], base=1, channel_multiplier=0)
    V("tensor_copy", out=self.ROW1[:], in_=itmp[:])
    G("iota", out=itmp[:], pattern=[[-1, 128]], base=128, channel_multiplier=0)
    V("tensor_copy", out=self.ROWB[:], in_=itmp[:])
    G("iota", out=itmp[:, 0:1], pattern=[[0, 1]], base=127, channel_multiplier=-1)
    V("tensor_copy", out=self.COLF[:], in_=itmp[:, 0:1])
    G("iota", out=itmp[:, 1:2], pattern=[[0, 1]], base=0, channel_multiplier=1)
    V("tensor_copy", out=self.COLB[:], in_=itmp[:, 1:2])
    G("iota", out=itmp[0:64, 2:3], pattern=[[0, 1]], base=0, channel_multiplier=1)
    G("iota", out=itmp[64:128, 2:3], pattern=[[0, 1]], base=0, channel_multiplier=1)
    V("tensor_copy", out=ftmp[:, 0:1], in_=itmp[:, 2:3])
    A(out=self.invf[:], in_=ftmp[:, 0:1], func=AF.Exp, scale=-math.log(10000.0) / 64.0)
    G("memset", ap=self.qsc[:], constant=-0.5 * math.log(128.0))
    self.BD16 = sbt("BD16", [128, 128]); self.MO32 = sbt("MO32", [128, 128]); self.MO64 = sbt("MO64", [128, 128]); self.MO128 = sbt("MO128", [128, 128])
    itf = sbt("itf", [128, 128], I32); itp = sbt("itp", [128, 128], I32); its = sbt("its", [128, 128], I32)
    ff = sbt("ff", [128, 128]); fp_ = sbt("fp_", [128, 128]); bd = [sbt("bd%d" % i, [128, 128]) for i in range(3)]
    G("iota", out=itf[:], pattern=[[1, 128]], base=0, channel_multiplier=0)
    G("iota", out=itp[:], pattern=[[0, 128]], base=0, channel_multiplier=1)
    for i, sh in enumerate((4, 5, 6)):
        V("tensor_scalar", out=its[:], in0=itf[:], scalar1=sh, scalar2=None, op0=ALU.arith_shift_right)
        V("tensor_copy", out=ff[:], in_=its[:])
        V("tensor_scalar", out=its[:], in0=itp[:], scalar1=sh, scalar2=None, op0=ALU.arith_shift_right)
        V("tensor_copy", out=fp_[:], in_=its[:])
        V("tensor_tensor", out=bd[i][:], in0=ff[:], in1=fp_[:], op=ALU.is_equal)
    V("tensor_copy", out=self.BD16[:], in_=bd[0][:])
    V("tensor_tensor", out=self.MO32[:], in0=bd[1][:], in1=bd[0][:], op=ALU.subtract)
    V("tensor_tensor", out=self.MO64[:], in0=bd[2][:], in1=bd[1][:], op=ALU.subtract)
    V("tensor_scalar", out=self.MO128[:], in0=bd[2][:], scalar1=-1.0, scalar2=1.0, op0=ALU.mult, op1=ALU.add)


def rot_tables(self):
    cfg, fw, nc = self.cfg, self.fw, self.nc
    S = cfg.S
    V, A, G = self.V, self.A, self.G
    self.ROT = nc.dram_tensor("ROT", [cfg.NS, 2, 128, S], F32, kind="Internal").ap()
    import os
    if os.environ.get("ROTSKIP"):
        return
    with ExitStack() as st:
        sb = lambda n, sh, dt=F32: st.enter_context(nc.sbuf_tensor(n + "__u%d" % self.uid(), list(sh), dt))
        posi = sb("posi", [128, TT], I32)
        ang = sb("ang", [128, TT]); kq = sb("kq", [128, TT]); kqi = sb("kqi", [128, TT], I32)
        sinT = sb("sinT", [128, TT]); cosT = sb("cosT", [128, TT])
        for b in range(cfg.NS):
            for ti in range(S // TT):
                t0 = ti * TT
                fw.dma("sync", posi[:], self.pos[b * S + t0:b * S + t0 + TT].partition_broadcast(128))
                V("tensor_copy", out=ang[:], in_=posi[:])
                V("tensor_scalar", out=ang[:], in0=ang[:], scalar1=self.invf[:, 0:1], scalar2=None, op0=ALU.mult)
                V("tensor_scalar", out=kq[:], in0=ang[:], scalar1=1.0 / (2 * PI), scalar2=None, op0=ALU.mult)
                V("tensor_copy", out=kqi[:], in_=kq[:])
                V("tensor_copy", out=kq[:], in_=kqi[:])
                V("scalar_tensor_tensor", out=ang[:], in0=kq[:], scalar=-2 * PI, in1=ang[:], op0=ALU.mult, op1=ALU.add)
                for dst, shift in ((sinT, 0.0), (cosT, PI / 2)):
                    V("tensor_scalar", out=kq[:], in0=ang[:], scalar1=shift, scalar2=None, op0=ALU.add)
                    V("tensor_scalar", out=dst[:], in0=kq[:], scalar1=PI, scalar2=-2 * PI, op0=ALU.is_gt, op1=ALU.mult)
                    V("tensor_tensor", out=kq[:], in0=kq[:], in1=dst[:], op=ALU.add)
                    V("tensor_scalar", out=dst[:], in0=kq[:], scalar1=-PI, scalar2=2 * PI, op0=ALU.is_lt, op1=ALU.mult)
                    V("tensor_tensor", out=kq[:], in0=kq[:], in1=dst[:], op=ALU.add)
                    V("tensor_scalar", out=kq[:], in0=kq[:], scalar1=PI, scalar2=-PI, op0=ALU.min, op1=ALU.max)
                    A(out=dst[:], in_=kq[:], func=AF.Sin)
                fw.dma("sync", self.ROT[b, 0, :, t0:t0 + TT], sinT[:], semkey="st_sin")
                fw.dma("sync", self.ROT[b, 1, :, t0:t0 + TT], cosT[:], semkey="st_cos")
    fw.fence()


def mix_phase(self, l):
    cfg, fw, nc = self.cfg, self.fw, self.nc
    S = cfg.S
    NCH = S // 128
    B = self.bank
    V, A, G = self.V, self.A, self.G
    KS = 128.0 ** -0.5
    if not hasattr(self, "RQ"):
        dr = lambda n, sh, dt=BF16: nc.dram_tensor(n, list(sh), dt, kind="Internal").ap()
        self.RQ = dr("RQ", [NCH, 128, 4, 128]); self.RK = dr("RK", [NCH, 128, 4, 128])
        self.RKt = dr("RKt", [NCH, 128, 4, 128]); self.RV = dr("RV", [NCH, 128, 512]); self.RGt = dr("RGt", [NCH, 128, 512])
        self.PRE = dr("PRE", [12, 128, S + 4], F32)
        self.GQ = dr("GQ", [NCH, 128, 4, 128]); self.GK = dr("GK", [NCH, 128, 4, 128])
        self.GKt = dr("GKt", [NCH, 128, 4, 128]); self.GV = dr("GV", [NCH, 128, 4, 128]); self.GZ = dr("GZ", [NCH, 128, 512])
        self.GB = dr("GB", [NCH, 128, 16], F32)
        self.OB = dr("OB", [NCH, 128, 1024], F32)
    win_bf = self.wbf["w_in"][l * D:(l + 1) * D, :].rearrange("(k p) n -> p k n", p=128)
    wout_bf = self.wbf["w_out"][l * D:(l + 1) * D, :].rearrange("(k p) n -> p k n", p=128)

    import os
    for b in range(cfg.NS if int(os.environ.get("MSTOP", "9")) > 0 else 0):
        fw.fence()
        with ExitStack() as st:
            sb = lambda n, sh, dt=F32: st.enter_context(nc.sbuf_tensor(n + "__u%d" % self.uid(), list(sh), dt))
            self.alloc_common(st, nxt=1, post=False)
            win = sb("win", [128, 8, INC], BF16)
            sinT = sb("sinT", [128, TT]); cosT = sb("cosT", [128, TT])
            gtmp = [sb("gtmp%d" % i, [128, TT]) for i in range(2)]
            xb = [sb("xb%d" % i, [128, TT], BF16) for i in range(2)]
            t1 = [sb("t1_%d" % i, [128, TT]) for i in range(2)]
            t2 = [sb("t2_%d" % i, [128, TT]) for i in range(2)]
            rq_t = sb("rq_t", [128, 4, 4, 128], BF16); rk_t = sb("rk_t", [128, 4, 4, 128], BF16)
            rkt_t = sb("rkt_t", [128, 4, 4, 128], BF16)
            rv_t = sb("rv_t", [128, 4, 512], BF16); rg_t = sb("rg_t", [128, 4, 512], BF16); gz_t = sb("gz_t", [128, 4, 512], BF16)
            gb_t = sb("gb_t", [128, 4, 16])
            pre_s = [sb("pre_s%d" % i, [128, TT]) for i in range(2)]
            zt = sb("zt", [128, 2])
            negA = sb("negA", [128, 8]); dtb = sb("dtb", [128, 8])
            sp = sb("sp", [128, 5, 8])
            self.load_norm_w(l, 1)
            fw.dma("sync", win[:, 0:4, :], win_bf[:, 0:4, :])
            fw.dma("sync", win[:, 4:8, :], win_bf[:, 4:8, :], semkey="win_b")
            fw.dma("sync", negA[:], self.a_log[l].partition_broadcast(128))
            fw.dma("sync", dtb[:], self.dt_bias[l].partition_broadcast(128))
            A(out=negA[:], in_=negA[:], func=AF.Exp)
            V("tensor_scalar", out=negA[:], in0=negA[:], scalar1=-1.0, scalar2=None, op0=ALU.mult)
            G("memset", ap=zt[:], constant=0.0)
            M1S = os.environ.get("M1S", "")
            for c in range(12 if "a" not in M1S else 0):
                fw.dma("sync", self.PRE[c, :, 0:2], zt[:, 0:2], semkey="prez")
                fw.dma("sync", self.PRE[c, :, S + 2:S + 4], zt[:, 0:2], semkey="prez")
            nxt = self.load_tile(b * S)
            for ti in range(S // TT):
                t0 = ti * TT
                xt = nxt
                self.norm_uT(xt, [B[6], B[7]])
                if ti + 1 < S // TT:
                    nxt = self.load_tile(b * S + t0 + TT)
                fw.dma("sync", sinT[:], self.ROT[b, 0, :, t0:t0 + TT], semkey="l_sin")
                fw.dma("sync", cosT[:], self.ROT[b, 1, :, t0:t0 + TT], semkey="l_cos")
                for fc in range(20 if "c" not in M1S else 0):
                    col0 = fc * 128 if fc < 8 else 2048 + (fc - 8) * 128
                    pb = B[fc % 2]
                    for k in range(8):
                        fw.matmul(pb[:], win[:, k, col0:col0 + 128], self.uT[:, k, :], start=(k == 0), stop=(k == 7))
                    if fc < 8 and "d" in M1S:
                        A(out=xb[fc % 2][:], in_=pb[:], func=AF.Copy)
                    elif fc < 8:
                        h = fc % 4
                        sc = 1.0 if fc < 4 else KS
                        x_b = xb[fc % 2]; a1 = t1[fc % 2]; a2 = t2[fc % 2]
                        V("scalar_tensor_tensor", out=a1[:], in0=pb[:], scalar=sc, in1=cosT[:], op0=ALU.mult, op1=ALU.mult)
                        A(out=x_b[:], in_=pb[:], func=AF.Copy, ins=[a1[:]])
                        px = B[2 + fc % 2]
                        fw.matmul(px[:], self.PTb[:], x_b[:])
                        V("scalar_tensor_tensor", out=a2[:], in0=px[:], scalar=sc, in1=sinT[:], op0=ALU.mult, op1=ALU.mult)
                        dst = rq_t if fc < 4 else rk_t
                        if "f" in M1S:
                            V("tensor_tensor", out=a1[:], in0=a1[:], in1=a2[:], op=ALU.add)
                        else:
                            V("tensor_tensor", out=dst[:, :, h, :], in0=a1[:, :].rearrange("p (s t) -> p s t", s=4),
                              in1=a2[:, :].rearrange("p (s t) -> p s t", s=4), op=ALU.add)
                        if fc >= 4 and "g" not in M1S:
                            tb = B[4 + fc % 2].bitcast(BF16)
                            for s in range(4):
                                fw.transpose(tb[:, s * 128:(s + 1) * 128], rk_t[:, s, h, :], self.identb[:])
                            A(out=rkt_t[:, :, h, :], in_=tb[:, 0:512].rearrange("p (s t) -> p s t", s=4), func=AF.Copy)
                    else:
                        c = fc - 8
                        ps_ = pre_s[fc % 2]
                        A(out=ps_[:], in_=pb[:], func=AF.Copy)
                        if "e" not in M1S:
                            fw.dma("sync", self.PRE[c, :, 2 + t0:2 + t0 + TT], ps_[:], semkey=("pre", fc % 2))
                n0 = ti * 4
                fw.dma("sync", self.RQ[n0:n0 + 4].rearrange("n p h t -> p n h t"), rq_t[:], semkey="st_rq")
                fw.dma("sync", self.RK[n0:n0 + 4].rearrange("n p h t -> p n h t"), rk_t[:], semkey="st_rk")
                fw.dma("sync", self.RKt[n0:n0 + 4].rearrange("n p h t -> p n h t"), rkt_t[:], semkey="st_rkt")
                for s in range(4 if "b" not in M1S else 0):
                    for kind, col0 in (("v", 1024), ("g", 1536), ("z", 3584)):
                        pb = B[{"v": 2, "g": 3, "z": 4}[kind]]
                        for k in range(8):
                            fw.matmul(pb[:], self.uT[:, k, s * 128:(s + 1) * 128], win[:, k, col0:col0 + 512], start=(k == 0), stop=(k == 7))
                        if kind == "v":
                            V("tensor_copy", out=rv_t[:, s, :], in_=pb[:])
                        else:
                            gt = gtmp[0 if kind == "g" else 1]
                            A(out=gt[:], in_=pb[:], func=AF.Tanh, scale=0.5)
                            V("scalar_tensor_tensor", out=gt[:], in0=gt[:], scalar=1.0, in1=pb[:], op0=ALU.add, op1=ALU.mult)
                            V("tensor_scalar", out=(rg_t if kind == "g" else gz_t)[:, s, :], in0=gt[:], scalar1=0.5, scalar2=None, op0=ALU.mult)
                    pb = B[5]
                    for k in range(8):
                        fw.matmul(pb[:, 0:16], self.uT[:, k, s * 128:(s + 1) * 128], win[:, k, 4096:4112], start=(k == 0), stop=(k == 7))
                    V("tensor_tensor", out=sp[:, 0, :], in0=pb[:, 0:8], in1=dtb[:], op=ALU.add)
                    V("scalar_tensor_tensor", out=sp[:, 1, :], in0=sp[:, 0, :], scalar=-1.0, in1=sp[:, 0, :], op0=ALU.mult, op1=ALU.min)
                    A(out=sp[:, 2, :], in_=sp[:, 1, :], func=AF.Exp, scale=1.0)
                    V("tensor_scalar", out=sp[:, 3, :], in0=sp[:, 2, :], scalar1=2.0, scalar2=None, op0=ALU.add)
                    V("reciprocal", out=sp[:, 3, :], in_=sp[:, 3, :])
                    V("tensor_tensor", out=sp[:, 1, :], in0=sp[:, 2, :], in1=sp[:, 3, :], op=ALU.mult)
                    V("tensor_tensor", out=sp[:, 2, :], in0=sp[:, 1, :], in1=sp[:, 1, :], op=ALU.mult)
                    V("tensor_scalar", out=sp[:, 3, :], in0=sp[:, 2, :], scalar1=1.0 / 13.0, scalar2=1.0 / 11.0, op0=ALU.mult, op1=ALU.add)
                    for cf in (1.0 / 9.0, 1.0 / 7.0, 1.0 / 5.0, 1.0 / 3.0, 1.0):
                        V("tensor_tensor", out=sp[:, 3, :], in0=sp[:, 3, :], in1=sp[:, 2, :], op=ALU.mult)
                        V("tensor_scalar", out=sp[:, 3, :], in0=sp[:, 3, :], scalar1=cf, scalar2=None, op0=ALU.add)
                    V("tensor_tensor", out=sp[:, 3, :], in0=sp[:, 3, :], in1=sp[:, 1, :], op=ALU.mult)
                    V("tensor_scalar", out=sp[:, 4, :], in0=sp[:, 0, :], scalar1=0.0, scalar2=None, op0=ALU.max)
                    V("scalar_tensor_tensor", out=sp[:, 4, :], in0=sp[:, 3, :], scalar=2.0, in1=sp[:, 4, :], op0=ALU.mult, op1=ALU.add)
                    V("tensor_tensor", out=gb_t[:, s, 0:8], in0=sp[:, 4, :], in1=negA[:], op=ALU.mult)
                    A(out=sp[:, 1, :], in_=pb[:, 8:16], func=AF.Tanh, scale=0.5, ins=[gb_t[:, s, 0:8]])
                    V("tensor_scalar", out=gb_t[:, s, 8:16], in0=sp[:, 1, :], scalar1=0.5, scalar2=0.5, op0=ALU.mult, op1=ALU.add)
                fw.dma("sync", self.RV[n0:n0 + 4].rearrange("n p f -> p n f"), rv_t[:], semkey="st_rv")
                fw.dma("sync", self.RGt[n0:n0 + 4].rearrange("n p f -> p n f"), rg_t[:], semkey="st_rg")
                fw.dma("sync", self.GZ[n0:n0 + 4].rearrange("n p f -> p n f"), gz_t[:], semkey="st_gz")
                fw.dma("sync", self.GB[n0:n0 + 4].rearrange("n p f -> p n f"), gb_t[:], semkey="st_gb")
        import os
        MSTOP = int(os.environ.get("MSTOP", "9"))
        if MSTOP <= 1:
            continue
        fw.fence()
        with ExitStack() as st:
            sb = lambda n, sh, dt=F32: st.enter_context(nc.sbuf_tensor(n + "__u%d" % self.uid(), list(sh), dt))
            cw = sb("cw", [128, 12, 5])
            pc = [sb("pc%d" % i, [128, TT + 4]) for i in range(2)]
            acc = [sb("acc%d" % i, [128, TT]) for i in range(2)]
            sil = [sb("sil%d" % i, [128, TT]) for i in range(2)]
            sq = [sb("sq%d" % i, [128, TT], BF16) for i in range(2)]
            rn = [sb("rn%d" % i, [128, TT]) for i in range(2)]
            vb_ = [sb("vb_%d" % i, [128, TT], BF16) for i in range(2)]
            gq_t = sb("gq_t", [128, 4, 4, 128], BF16); gk_t = sb("gk_t", [128, 4, 4, 128], BF16)
            gkt_t = sb("gkt_t", [128, 4, 4, 128], BF16); gv_t = sb("gv_t", [128, 4, 4, 128], BF16)
            for k_ in range(5):
                fw.dma("sync", cw[:, :, k_], self.gdn_conv[l * 5 + k_, :].rearrange("(c p) -> p c", p=128), semkey="cw")
            sil8 = [[sb("sil8_%d_%d" % (j, i), [128, TT]) for i in range(8)] for j in range(2)]
            rn8 = [sb("rn8_%d" % j, [8, TT]) for j in range(2)]
            gv_t2 = [gv_t, sb("gv_t2", [128, 4, 4, 128], BF16)]
            OH = sb("OH", [128, 8, 8], BF16)
            SEL = sb("SEL", [8, 8, 128])
            ones8 = sb("ones8", [8, 8, 128])
            V("memset", ap=OH[:], constant=0.0)
            for c in range(8):
                V("memset", ap=OH[:, c, c:c + 1], constant=1.0)
            G("memset", ap=ones8[:], constant=1.0)
            G("affine_select", out=SEL[:], in_=ones8[:], pattern=[[1, 8], [0, 128]], compare_op=ALU.is_equal, fill=0.0,
              base=0, channel_multiplier=-1)
            cstate = {"ci": 0}

            def passA(ti):
                t0 = ti * TT
                j2 = ti % 2
                for c in range(12):
                    h = c % 4
                    i2 = cstate["ci"] % 2
                    cstate["ci"] += 1
                    fw.dma("sync", pc[i2][:], self.PRE[c, :, t0:t0 + TT + 4], semkey=("pc", i2))
                    V("tensor_scalar", out=acc[i2][:], in0=pc[i2][:, 0:TT], scalar1=cw[:, c, 0:1], scalar2=None, op0=ALU.mult)
                    for j in range(1, 5):
                        V("scalar_tensor_tensor", out=acc[i2][:], in0=pc[i2][:, j:j + TT], scalar=cw[:, c, j:j + 1], in1=acc[i2][:],
                          op0=ALU.mult, op1=ALU.add)
                    sl = sil8[j2][c] if c < 8 else sil[i2]
                    A(out=sl[:], in_=acc[i2][:], func=AF.Tanh, scale=0.5)
                    V("scalar_tensor_tensor", out=sl[:], in0=sl[:], scalar=1.0, in1=acc[i2][:], op0=ALU.add, op1=ALU.mult)
                    if c < 8:
                        A(out=sq[i2][:], in_=sl[:], func=AF.Square)
                        fw.matmul(B[0][0:8, :], OH[:, c, :], sq[i2][:], start=(c == 0), stop=(c == 7))
                        if c == 7:
                            V("tensor_scalar", out=rn8[j2][:], in0=B[0][0:8, :], scalar1=4.0 * 1e-6, scalar2=None, op0=ALU.add)
                            G("tensor_tensor", out=rn8[j2][:], in0=rn8[j2][:], in1=self.mhalf[0:8, :], op=ALU.pow)
                    else:
                        V("tensor_scalar", out=vb_[i2][:], in0=sl[:], scalar1=0.5, scalar2=None, op0=ALU.mult)
                        tb = B[6 + c % 2].bitcast(BF16)
                        for s in range(4):
                            fw.transpose(tb[:, s * 128:(s + 1) * 128], vb_[i2][:, s * 128:(s + 1) * 128], self.identb[:])
                        V("tensor_copy", out=gv_t2[j2][:, :, h, :], in_=tb[:, 0:512].rearrange("p (s t) -> p s t", s=4))

            def passB(ti):
                j2 = ti % 2
                for c in range(8):
                    h = c % 4
                    pn = B[1 + c % 2]
                    fw.matmul(pn[:], SEL[:, c, :], rn8[j2][:])
                    dst = gq_t if c < 4 else gk_t
                    V("scalar_tensor_tensor", out=dst[:, :, h, :], in0=sil8[j2][c][:, :].rearrange("p (s t) -> p s t", s=4),
                      scalar=(KS if c < 4 else 1.0), in1=pn[:, :].rearrange("p (s t) -> p s t", s=4), op0=ALU.mult, op1=ALU.mult)
                    if c >= 4:
                        tb = B[4 + c % 2].bitcast(BF16)
                        for s in range(4):
                            fw.transpose(tb[:, s * 128:(s + 1) * 128], gk_t[:, s, h, :], self.identb[:])
                        A(out=gkt_t[:, :, h, :], in_=tb[:, 0:512].rearrange("p (s t) -> p s t", s=4), func=AF.Copy)
                n0 = ti * 4
                fw.dma("sync", self.GQ[n0:n0 + 4].rearrange("n p h t -> p n h t"), gq_t[:], semkey="st_gq")
                fw.dma("sync", self.GK[n0:n0 + 4].rearrange("n p h t -> p n h t"), gk_t[:], semkey="st_gk")
                fw.dma("sync", self.GKt[n0:n0 + 4].rearrange("n p h t -> p n h t"), gkt_t[:], semkey="st_gkt")
                fw.dma("sync", self.GV[n0:n0 + 4].rearrange("n p h t -> p n h t"), gv_t2[j2][:], semkey=("st_gv", j2))

            passA(0)
            for ti in range(S // TT):
                if ti + 1 < S // TT:
                    passA(ti + 1)
                passB(ti)
        if MSTOP <= 2:
            continue
        fw.fence()
        with ExitStack() as st:
            sb = lambda n, sh, dt=F32: st.enter_context(nc.sbuf_tensor(n + "__u%d" % self.uid(), list(sh), dt))
            T = type("T", (), {})()
            lg = sb("lg", [128, 8]); cd = sb("cd", [128, 8]); kdc = sb("kdc", [128, 8])
            DT = sb("DT", [128, 4, 128]); QDF = sb("QDF", [128, 4, 128]); QDB = sb("QDB", [128, 4, 128])
            e1 = sb("e1", [128, 128]); e2 = sb("e2", [128, 128])
            gnw = sb("gnw", [128, 128])
            wout = sb("wout", [128, 8, D], BF16)
            wpost = sb("wpost", [128, D])
            fw.dma("sync", lg[:], self.ret_lg[l].partition_broadcast(128))
            fw.dma("sync", gnw[:], self.gdn_norm[l].partition_broadcast(128))
            fw.dma("sync", wout[:], wout_bf)
            fw.dma("sync", wpost[:], self.norm_post[l * 4 + 1].partition_broadcast(128))
            self.wpost = wpost
            A(out=cd[:], in_=lg[:], func=AF.Exp, scale=128.0)
            for h in range(4):
                V("tensor_scalar", out=e1[:], in0=self.POS[:], scalar1=lg[:, h:h + 1], scalar2=None, op0=ALU.mult)
                V("scalar_tensor_tensor", out=e2[:], in0=self.NEG[:], scalar=lg[:, 4 + h:5 + h], in1=e1[:], op0=ALU.mult, op1=ALU.add)
                A(out=DT[:, h, :], in_=e2[:], func=AF.Exp)
                A(out=QDF[:, h, :], in_=self.ROW1[:], func=AF.Exp, scale=lg[:, h:h + 1])
                A(out=QDB[:, h, :], in_=self.ROWB[:], func=AF.Exp, scale=lg[:, 4 + h:5 + h])
                A(out=kdc[:, h:h + 1], in_=self.COLF[:], func=AF.Exp, scale=lg[:, h:h + 1])
                A(out=kdc[:, 4 + h:5 + h], in_=self.COLB[:], func=AF.Exp, scale=lg[:, 4 + h:5 + h])
            bft = lambda n: sb(n, [128, 4, 128], BF16)
            f32t = lambda n: sb(n, [128, 4, 128])
            rq = [bft("rq%d" % i) for i in range(2)]; rk = [bft("rk%d" % i) for i in range(2)]
            rkt = [bft("rkt%d" % i) for i in range(2)]; rv = [bft("rv%d" % i) for i in range(2)]
            rg = [sb("rg%d" % i, [128, 512], BF16) for i in range(2)]
            gq = [bft("gq%d" % i) for i in range(2)]; gk = [bft("gk%d" % i) for i in range(2)]
            gkt = [bft("gkt%d" % i) for i in range(2)]; gv = [bft("gv%d" % i) for i in range(2)]
            gz = [sb("gz%d" % i, [128, 512], BF16) for i in range(2)]
            gbt = [sb("gbt%d" % i, [128, 16]) for i in range(2)]
            obt = [sb("obt%d" % i, [128, 1024]) for i in range(2)]
            xc = [sb("xc%d" % i, [128, 1, D]) for i in range(2)]
            Sr = f32t("Sr"); Srb = bft("Srb"); Sg = f32t("Sg"); Sgb = bft("Sgb")
            qd = bft("qd"); kd = bft("kd"); sT = bft("sT")
            T.Gc = sb("Gc", [128, 4]); T.nGc = sb("nGc", [128, 4]); T.nbeta = sb("nbeta", [128, 4])
            T.dg = f32t("dg"); T.db = f32t("db"); T.eG = f32t("eG"); T.tE = f32t("tE"); T.tET = f32t("tET")
            T.E = f32t("E"); T.ET = f32t("ET"); T.Es = f32t("Es"); T.ETs = f32t("ETs"); T.tq = f32t("tq")
            T.kg = bft("kg"); T.qg = bft("qg"); T.QKET = bft("QKET")
            T.P0 = f32t("P0"); T.Q0 = f32t("Q0")
            T.Pd = [f32t("Pd%d" % i) for i in range(2)]; T.Qd = [f32t("Qd%d" % i) for i in range(2)]
            T.X = [f32t("X%d" % i) for i in range(2)]; T.XT = [f32t("XT%d" % i) for i in range(2)]
            T.Y = f32t("Yg"); T.YT = f32t("YTg"); T.O = f32t("Og"); T.OT = f32t("OTg")
            T.rr = bft("rr"); T.vn = bft("vn"); T.kt = bft("kt")
            otot = sb("otot", [128, 1024]); og = sb("og", [128, 1024], BF16); ogT = sb("ogT", [128, 8, 128], BF16)
            st6 = sb("st6", [128, 4, 6]); mv = sb("mv", [128, 4, 2]); rstd = sb("rstd", [128, 8]); tmpn = sb("tmpn", [128, 512])
            ssg = sb("ssg", [128, 4])
            self.ss2 = sb("ss2m", [128, 4]); self.rs2 = sb("rs2m", [128, 4])
            self.tpost = [sb("tpostm%d" % i, [128, D]) for i in range(2)]

            Hh = []
            for i in range(2):
                hh = type("H", (), {})()
                hh.kg = bft("Hkg%d" % i); hh.qg = bft("Hqg%d" % i); hh.QKET = bft("HQK%d" % i); hh.R = bft("HR%d" % i); hh.kt = bft("Hkt%d" % i)
                hh.gl = sb("Hgl%d" % i, [128, 4])
                Hh.append(hh)
            v4 = lambda bank: bank[:, :].rearrange("p (h t) -> p h t", h=4)

            def gdn_prep(r, H, gq_, gk_, gkt_, gb_):
                g = gb_[:, r * 4:(r + 1) * 4]
                beta = gb_[:, 8 + r * 4:8 + (r + 1) * 4]
                tri = self.UP1 if r == 0 else self.LO1
                NMi = self.NM_lo if r == 0 else self.NM_up
                NMj = self.NM_up if r == 0 else self.NM_lo
                last = 127 if r == 0 else 0
                fw.matmul(B[5][:, 0:4], tri[:], g)
                V("tensor_copy", out=T.Gc[:], in_=B[5][:, 0:4])
                V("tensor_scalar", out=T.nGc[:], in0=T.Gc[:], scalar1=-1.0, scalar2=None, op0=ALU.mult)
                V("tensor_scalar", out=T.nbeta[:], in0=beta, scalar1=-1.0, scalar2=None, op0=ALU.mult)
                V("tensor_tensor", out=T.dg[:], in0=bc_mid(self.identf[:, :], 4), in1=bc_last(T.Gc[:, 0:4], 128), op=ALU.mult)
                V("tensor_tensor", out=T.db[:], in0=bc_mid(self.identf[:, :], 4), in1=bc_last(beta, 128), op=ALU.mult)
                yield
                fw.matmul(B[3][:], self.onesf[:], T.dg[:, :, :].rearrange("p h t -> p (h t)"))
                fw.matmul(B[4][:], self.onesf[:], T.db[:, :, :].rearrange("p h t -> p (h t)"))
                A(out=T.eG[:], in_=v4(B[3]), func=AF.Exp)
                yield
                V("tensor_tensor", out=H.kg[:], in0=gk_[:], in1=T.eG[:], op=ALU.mult)
                V("tensor_tensor", out=H.qg[:], in0=gq_[:], in1=T.eG[:], op=ALU.mult)
                V("tensor_copy", out=H.gl[:], in_=T.eG[:, :, last])
                yield
                V("scalar_tensor_tensor", out=T.tE[:], in0=v4(B[3]), scalar=-1.0, in1=bc_mid(NMi[:, :], 4), op0=ALU.mult, op1=ALU.add,
                  ins=[T.eG[:]])
                V("tensor_tensor", out=T.tET[:], in0=v4(B[3]), in1=bc_mid(NMj[:, :], 4), op=ALU.add)
                yield
                for h in range(4):
                    A(out=T.E[:, h, :], in_=T.tE[:, h, :], func=AF.Exp, bias=T.Gc[:, h:h + 1], scale=1.0)
                    A(out=T.ET[:, h, :], in_=T.tET[:, h, :], func=AF.Exp, bias=T.nGc[:, h:h + 1], scale=1.0)
                yield
                V("tensor_tensor", out=T.Es[:], in0=T.E[:], in1=bc_mid(self.OFFD[:, :], 4), op=ALU.mult)
                V("tensor_tensor", out=T.ETs[:], in0=T.ET[:], in1=bc_mid(self.OFFD[:, :], 4), op=ALU.mult)
                V("tensor_tensor", out=H.kt[:], in0=gkt_[:], in1=T.ET[:, :, last:last + 1].broadcast_to([128, 4, 128]), op=ALU.mult)
                yield
                for h in range(4):
                    fw.matmul(B[2][:, h * 128:(h + 1) * 128], gk_[:, h, :], gk_[:, h, :])
                for h in range(4):
                    fw.matmul(B[3][:, h * 128:(h + 1) * 128], gk_[:, h, :], gq_[:, h, :])
                yield
                V("tensor_tensor", out=H.QKET[:], in0=v4(B[3]), in1=T.ET[:], op=ALU.mult)
                for h in range(4):
                    V("scalar_tensor_tensor", out=T.P0[:, h, :], in0=B[2][:, h * 128:(h + 1) * 128], scalar=T.nbeta[:, h:h + 1],
                      in1=T.ETs[:, h, :], op0=ALU.mult, op1=ALU.mult)
                yield
                V("tensor_tensor", out=T.tq[:], in0=T.Es[:], in1=v4(B[4]), op=ALU.mult)
                V("scalar_tensor_tensor", out=T.Q0[:], in0=v4(B[2]), scalar=-1.0, in1=T.tq[:], op0=ALU.mult, op1=ALU.mult)
                yield
                Pd, Qd, X, XT = T.Pd, T.Qd, T.X, T.XT
                V("tensor_tensor", out=Pd[0][:], in0=T.P0[:], in1=bc_mid(self.BD16[:, :], 4), op=ALU.mult)
                V("tensor_tensor", out=Qd[0][:], in0=T.Q0[:], in1=bc_mid(self.BD16[:, :], 4), op=ALU.mult)
                yield
                V("tensor_tensor", out=X[0][:], in0=Pd[0][:], in1=bc_mid(self.identf[:, :], 4), op=ALU.add)
                V("tensor_tensor", out=XT[0][:], in0=Qd[0][:], in1=bc_mid(self.identf[:, :], 4), op=ALU.add)
                yield
                xi = 0
                for k in range(1, 4):
                    a, bb = (k - 1) % 2, k % 2
                    for h in range(4):
                        fw.matmul(B[5][:, h * 128:(h + 1) * 128], Qd[a][:, h, :], Pd[a][:, h, :])
                    for h in range(4):
                        fw.matmul(B[6][:, h * 128:(h + 1) * 128], Pd[a][:, h, :], Qd[a][:, h, :])
                    yield
                    A(out=Pd[bb][:], in_=v4(B[5]), func=AF.Copy)
                    V("tensor_copy", out=Qd[bb][:], in_=v4(B[6]))
                    yield
                    for h in range(4):
                        fw.matmul(B[7][:, h * 128:(h + 1) * 128], Qd[bb][:, h, :], X[xi][:, h, :])
                    for h in range(4):
                        fw.matmul(B[4][:, h * 128:(h + 1) * 128], Pd[bb][:, h, :], XT[xi][:, h, :])
                    yield
                    V("tensor_tensor", out=X[1 - xi][:], in0=X[xi][:], in1=v4(B[7]), op=ALU.add)
                    V("tensor_tensor", out=XT[1 - xi][:], in0=XT[xi][:], in1=v4(B[4]), op=ALU.add)
                    yield
                    xi = 1 - xi
                for li, MO in enumerate((self.MO32, self.MO64, self.MO128)):
                    V("tensor_tensor", out=T.O[:], in0=T.P0[:], in1=bc_mid(MO[:, :], 4), op=ALU.mult)
                    V("tensor_tensor", out=T.OT[:], in0=T.Q0[:], in1=bc_mid(MO[:, :], 4), op=ALU.mult)
                    yield
                    for h in range(4):
                        fw.matmul(B[5][:, h * 128:(h + 1) * 128], T.OT[:, h, :], X[xi][:, h, :])
                    if li < 2:
                        for h in range(4):
                            fw.matmul(B[6][:, h * 128:(h + 1) * 128], T.O[:, h, :], XT[xi][:, h, :])
                    yield
                    A(out=T.Y[:], in_=v4(B[5]), func=AF.Copy)
                    if li < 2:
                        V("tensor_copy", out=T.YT[:], in_=v4(B[6]))
                    yield
                    for h in range(4):
                        fw.matmul(B[7][:, h * 128:(h + 1) * 128], XT[xi][:, h, :], T.Y[:, h, :])
                    if li < 2:
                        for h in range(4):
                            fw.matmul(B[4][:, h * 128:(h + 1) * 128], X[xi][:, h, :], T.YT[:, h, :])
                    yield
                    V("tensor_tensor", out=X[1 - xi][:], in0=X[xi][:], in1=v4(B[7]), op=ALU.add)
                    if li < 2:
                        V("tensor_tensor", out=XT[1 - xi][:], in0=XT[xi][:], in1=v4(B[4]), op=ALU.add)
                    yield
                    xi = 1 - xi
                V("tensor_copy", out=H.R[:], in_=X[xi][:])
                yield

            def gdn_scan(r, H, gv_, gb_, S32, Sbf, obank):
                beta = gb_[:, 8 + r * 4:8 + (r + 1) * 4]
                for h in range(4):
                    fw.matmul(B[1][:, h * 128:(h + 1) * 128], H.kg[:, h, :], Sbf[:, h, :])
                yield
                V("tensor_tensor", out=T.rr[:], in0=gv_[:], in1=v4(B[1]), op=ALU.subtract)
                yield
                for h in range(4):
                    fw.matmul(B[1][:, h * 128:(h + 1) * 128], H.R[:, h, :], T.rr[:, h, :])
                yield
                V("tensor_tensor", out=T.vn[:], in0=v4(B[1]), in1=bc_last(beta, 128), op=ALU.mult)
                yield
                for h in range(4):
                    fw.matmul(obank[:, h * 128:(h + 1) * 128], H.qg[:, h, :], Sbf[:, h, :], start=True, stop=False)
                    fw.matmul(obank[:, h * 128:(h + 1) * 128], H.QKET[:, h, :], T.vn[:, h, :], start=False, stop=True)
                for h in range(4):
                    fw.matmul(B[1][:, h * 128:(h + 1) * 128], H.kt[:, h, :], T.vn[:, h, :])
                yield
                for h in range(4):
                    V("scalar_tensor_tensor", out=S32[:, h, :], in0=S32[:, h, :], scalar=H.gl[:, h:h + 1],
                      in1=B[1][:, h * 128:(h + 1) * 128], op0=ALU.mult, op1=ALU.add)
                V("tensor_copy", out=Sbf[:], in_=S32[:])
                yield

            def ret_state(r, rkt_, rv_, S32, Sbf):
                V("tensor_tensor", out=kd[:], in0=rkt_[:], in1=bc_last(kdc[:, r * 4:(r + 1) * 4], 128), op=ALU.mult)
                for h in range(4):
                    fw.matmul(B[1][:, h * 128:(h + 1) * 128], kd[:, h, :], rv_[:, h, :])
                for h in range(4):
                    V("scalar_tensor_tensor", out=S32[:, h, :], in0=S32[:, h, :], scalar=cd[:, r * 4 + h:r * 4 + h + 1],
                      in1=B[1][:, h * 128:(h + 1) * 128], op0=ALU.mult, op1=ALU.add)
                V("tensor_copy", out=Sbf[:], in_=S32[:])

            def zero_states():
                for t_ in (Sr, Sg):
                    V("memset", ap=t_[:], constant=0.0)
                for t_ in (Srb, Sgb):
                    V("memset", ap=t_[:], constant=0.0)

            def drain(g):
                for _ in g:
                    pass

            def interleave(g1, g2):
                gens = [g for g in (g1, g2) if g is not None]
                while gens:
                    for g in list(gens):
                        try:
                            next(g)
                        except StopIteration:
                            gens.remove(g)

            def prep_stream(sweep, it, n):
                i2 = it % 2
                r = 1 if sweep == 2 else 0
                fw.dma("sync", gq[i2][:], self.GQ[n], semkey=("l_gq", i2))
                fw.dma("sync", gk[i2][:], self.GK[n], semkey=("l_gk", i2))
                fw.dma("sync", gkt[i2][:], self.GKt[n], semkey=("l_gkt", i2))
                fw.dma("sync", gv[i2][:], self.GV[n], semkey=("l_gv", i2))
                fw.dma("sync", gbt[i2][:], self.GB[n], semkey=("l_gb", i2))
                fw.dma("sync", rq[i2][:], self.RQ[n], semkey=("l_rq", i2))
                fw.dma("sync", rkt[i2][:], self.RKt[n], semkey=("l_rkt", i2))
                fw.dma("sync", rv[i2][:], self.RV[n].rearrange("p (h e) -> p h e", h=4), semkey=("l_rv", i2))
                if sweep == 3:
                    fw.dma("sync", rk[i2][:], self.RK[n], semkey=("l_rk", i2))
                    fw.dma("sync", rg[i2][:], self.RGt[n], semkey=("l_rg", i2))
                    fw.dma("sync", gz[i2][:], self.GZ[n], semkey=("l_gz", i2))
                    fw.dma("sync", obt[i2][:], self.OB[n], semkey=("l_ob", i2))
                    row0 = b * S + n * 128
                    fw.dma("sync", xc[i2][:, 0, :], self.y[row0:row0 + 128, :], semkey=("l_xc", i2))
                yield
                yield from gdn_prep(r, Hh[i2], gq[i2], gk[i2], gkt[i2], gbt[i2])

            def m2_stream(it, n):
                i2 = it % 2
                V("tensor_tensor", out=qd[:], in0=rq[i2][:], in1=QDB[:], op=ALU.mult)
                for h in range(4):
                    fw.matmul(B[0][:, h * 128:(h + 1) * 128], qd[:, h, :], Srb[:, h, :])
                yield
                A(out=obt[i2][:, 0:512], in_=B[0][:], func=AF.Copy)
                ret_state(1, rkt[i2], rv[i2], Sr, Srb)
                yield
                yield from gdn_scan(1, Hh[i2], gv[i2], gbt[i2], Sg, Sgb, B[0])
                A(out=obt[i2][:, 512:1024], in_=B[0][:], func=AF.Copy)
                fw.dma("sync", self.OB[n], obt[i2][:], semkey=("st_ob", i2))
                yield

            def m3_stream(it, n):
                i2 = it % 2
                row0 = b * S + n * 128
                for h in range(4):
                    fw.matmul(B[0][:, h * 128:(h + 1) * 128], rk[i2][:, h, :], rq[i2][:, h, :])
                yield
                V("tensor_tensor", out=sT[:], in0=B[0][:, :].rearrange("p (h t) -> p h t", h=4), in1=DT[:], op=ALU.mult)
                V("tensor_tensor", out=qd[:], in0=rq[i2][:], in1=QDF[:], op=ALU.mult)
                yield
                for h in range(4):
                    fw.matmul(B[0][:, h * 128:(h + 1) * 128], sT[:, h, :], rv[i2][:, h, :], start=True, stop=False)
                    fw.matmul(B[0][:, h * 128:(h + 1) * 128], qd[:, h, :], Srb[:, h, :], start=False, stop=True)
                yield
                V("tensor_tensor", out=otot[:, 0:512], in0=B[0][:], in1=obt[i2][:, 0:512], op=ALU.add)
                ret_state(0, rkt[i2], rv[i2], Sr, Srb)
                yield
                for h in range(4):
                    V("bn_stats", out=st6[:, h, :], in_=otot[:, h * 128:(h + 1) * 128])
                    V("bn_aggr", out=mv[:, h, :], in_=st6[:, h, :])
                V("tensor_scalar", out=rstd[:, 0:4], in0=mv[:, :, 1], scalar1=1e-6, scalar2=None, op0=ALU.add)
                G("tensor_tensor", out=rstd[:, 0:4], in0=rstd[:, 0:4], in1=self.mhalf[:, 0:4], op=ALU.pow)
                yield
                for h in range(4):
                    V("tensor_scalar", out=tmpn[:, h * 128:(h + 1) * 128], in0=otot[:, h * 128:(h + 1) * 128],
                      scalar1=mv[:, h, 0:1], scalar2=rstd[:, h:h + 1], op0=ALU.subtract, op1=ALU.mult)
                V("tensor_tensor", out=og[:, 0:512], in0=tmpn[:], in1=rg[i2][:], op=ALU.mult)
                yield
                yield from gdn_scan(0, Hh[i2], gv[i2], gbt[i2], Sg, Sgb, B[0])
                V("tensor_tensor", out=otot[:, 512:1024], in0=B[0][:], in1=obt[i2][:, 512:1024], op=ALU.add)
                for h in range(4):
                    A(out=self.junk[:, h * 128:(h + 1) * 128], in_=otot[:, 512 + h * 128:512 + (h + 1) * 128], func=AF.Square,
                      accum_out=ssg[:, h:h + 1])
                yield
                V("tensor_scalar", out=rstd[:, 4:8], in0=ssg[:], scalar1=1.0 / 128.0, scalar2=1e-6, op0=ALU.mult, op1=ALU.add)
                G("tensor_tensor", out=rstd[:, 4:8], in0=rstd[:, 4:8], in1=self.mhalf[:, 0:4], op=ALU.pow)
                yield
                for h in range(4):
                    V("scalar_tensor_tensor", out=tmpn[:, h * 128:(h + 1) * 128], in0=otot[:, 512 + h * 128:512 + (h + 1) * 128],
                      scalar=rstd[:, 4 + h:5 + h], in1=gnw[:], op0=ALU.mult, op1=ALU.mult)
                V("tensor_tensor", out=og[:, 512:1024], in0=tmpn[:], in1=gz[i2][:], op=ALU.mult)
                yield
                tb = B[1].bitcast(BF16)
                for j in range(8):
                    fw.transpose(tb[:, j * 128:(j + 1) * 128], og[:, j * 128:(j + 1) * 128], self.identb[:])
                yield
                A(out=ogT[:], in_=tb[:, :].rearrange("p (j t) -> p j t", j=8), func=AF.Copy)
                yield
                for hf, pb in ((0, B[0]), (1, B[1])):
                    for k in range(8):
                        fw.matmul(pb[:], ogT[:, k, :], wout[:, k, hf * 512:(hf + 1) * 512], start=(k == 0), stop=(k == 7))
                yield
                self.post_resid(xc[i2], 0, B[0], B[1], 1.0)
                fw.dma("sync", self.y[row0:row0 + 128, :], xc[i2][:, 0, :], semkey=("st_y", i2))
                yield

            for sweep, order, stream in ((2, list(reversed(range(NCH))), m2_stream), (3, list(range(NCH)), m3_stream)):
                zero_states()
                drain(prep_stream(sweep, 0, order[0]))
                for it, n in enumerate(order):
                    nxt = prep_stream(sweep, it + 1, order[it + 1]) if it + 1 < len(order) else None
                    interleave(stream(it, n), nxt)
    fw.fence()


K.mix_phase = mix_phase
K.mix_consts = mix_consts
K.rot_tables = rot_tables


def make_in_maps(inp, cfg, ncores):
    L = cfg.DEPTH
    maps = []
    for c in range(ncores):
        b0 = c * cfg.NS
        m = {
            "x": np.ascontiguousarray(inp["x"][b0:b0 + cfg.NS]).reshape(cfg.NT, D),
            "mem": np.ascontiguousarray(inp["mem"][b0:b0 + cfg.NS]).reshape(cfg.NS * MEM, D),
            "positions": np.ascontiguousarray(inp["positions"][b0:b0 + cfg.NS]).reshape(cfg.NT),
            "norm_pre": inp["norm_pre"][:L].reshape(L * 4, D),
            "norm_post": inp["norm_post"][:L].reshape(L * 4, D),
            "mem_norm": inp["mem_norm"][:L].reshape(L, D),
            "gdn_conv": inp["gdn_conv"][:L].reshape(L * 5, 1536),
            "ret_log_gamma": inp["ret_log_gamma"][:L].reshape(L, 8),
            "gdn_a_log": inp["gdn_a_log"][:L].reshape(L, 8),
            "gdn_dt_bias": inp["gdn_dt_bias"][:L].reshape(L, 8),
            "gdn_norm": inp["gdn_norm"][:L].reshape(L, 128),
        }
        for n, r, cc in WNAMES:
            m[n] = inp[n][:L].reshape(L * r, cc)
        maps.append({k: np.ascontiguousarray(v) for k, v in m.items()})
    return maps


def kernel(**inputs):
    cfg = Cfg(NS=2, S=4096, DEPTH=4)
    inp = {k: np.asarray(v) for k, v in inputs.items()}
    k = K(cfg)
    k.build()
    in_maps = make_in_maps(inp, cfg, 8)
    res = run_bass_kernel_spmd(k.nc, in_maps, core_ids=list(range(8)))
    out = np.concatenate([np.asarray(r["y"]).reshape(cfg.NS, cfg.S, D) for r in res.results], axis=0)
    return out.astype(np.float32)
```
